# Optimizing a Trainium2 kernel written in Bass

```python
import math
import jax, jax.numpy as jnp
from jax import lax
import numpy as np

D_MODEL = 1024
BATCH = 32
SEQ = 256
DEPTH = 2
DEC_BATCH = 8
DEC_SEQ = 1024
PAST_LEN = 512

GRID_W = 64
HEAD_DIM = 64
A_HEADS = 8
B_HEADS = 8
B_KV_HEADS = 2
C_HEADS = 16
C_KV_HEADS = 4
NA_WIN_H = 8
NA_WIN_W = 16
B_WINDOW = 128
Q_BLOCK = 128
D_FF = ((8 * D_MODEL // 3 + 255) // 256) * 256
ROPE_THETA = 10000.0
RMS_EPS = 1e-6
NEG_INF = -1e30
N_EVEN = (DEPTH + 1) // 2
N_ODD = DEPTH // 2
EVEN_HEADS = (A_HEADS, A_HEADS, A_HEADS, B_HEADS, B_KV_HEADS, B_KV_HEADS)
ODD_HEADS = (C_HEADS, C_KV_HEADS, C_KV_HEADS)
EVEN_IN = sum(EVEN_HEADS) * HEAD_DIM
ODD_IN = sum(ODD_HEADS) * HEAD_DIM
EVEN_OUT = (A_HEADS + B_HEADS) * HEAD_DIM
ODD_OUT = C_HEADS * HEAD_DIM

kernel_name = 'hybrid_prefix_diffusion_step'


def rms_norm(x, g):
    xf = x.astype(jnp.float32)
    y = xf * lax.rsqrt(jnp.mean(xf * xf, axis=-1, keepdims=True) + RMS_EPS)
    return y.astype(x.dtype) * g


def modulation(cvec, w_mod, b_mod):
    m = jax.nn.silu(cvec) @ w_mod + b_mod
    return jnp.split(m[:, None, :], 6, axis=-1)


def adaln(x, g, shift, scale):
    return rms_norm(x, g) * (1 + scale) + shift


def split_heads(p, heads):
    b, n, _ = p.shape
    bounds = [int(i) for i in np.cumsum([h * HEAD_DIM for h in heads])[:-1]]
    parts = jnp.split(p, bounds, axis=-1)
    return [t.reshape(b, n, h, HEAD_DIM) for t, h in zip(parts, heads)]


def axial_rope(x):
    n = x.shape[1]
    t = jnp.arange(n)
    row = (t // GRID_W).astype(jnp.float32)
    col = (t % GRID_W).astype(jnp.float32)
    half = HEAD_DIM // 2
    inv_freq = ROPE_THETA ** (-jnp.arange(0, half, 2, dtype=jnp.float32) / half)

    def rot(xa, pos):
        ang = pos[:, None] * inv_freq[None, :]
        cos = jnp.cos(ang)[None, :, None, :].astype(x.dtype)
        sin = jnp.sin(ang)[None, :, None, :].astype(x.dtype)
        x1, x2 = xa[..., :half // 2], xa[..., half // 2:]
        return jnp.concatenate([x1 * cos - x2 * sin, x2 * cos + x1 * sin], axis=-1)

    return jnp.concatenate([rot(x[..., :half], row), rot(x[..., half:], col)], axis=-1)


def softmax_sink(s, sink):
    if sink is None:
        return jax.nn.softmax(s, axis=-1)
    sk = jnp.broadcast_to(sink.astype(jnp.float32), s.shape[:-1] + (1,))
    p = jax.nn.softmax(jnp.concatenate([s, sk], axis=-1), axis=-1)
    return p[..., :-1]


def block_attention(q, k, v, sink=None):
    b, nq, hq, d = q.shape
    hkv = k.shape[2]
    g = hq // hkv
    nb = nq // Q_BLOCK
    scale = 1.0 / math.sqrt(d)
    qb = q.reshape(b, nb, Q_BLOCK, hkv, g, d).transpose(1, 0, 2, 3, 4, 5)
    sink_b = None if sink is None else sink.reshape(hkv, g)[None, :, :, None, None]

    def one(qblk):
        s = jnp.einsum('bqhgd,bkhd->bhgqk', qblk, k).astype(jnp.float32) * scale
        p = softmax_sink(s, sink_b)
        return jnp.einsum('bhgqk,bkhd->bqhgd', p.astype(v.dtype), v)

    out = lax.map(one, qb)
    return out.transpose(1, 0, 2, 3, 4, 5).reshape(b, nq, hq, d)


def window_attention(q, k, v, k_ctx, v_ctx, sink):
    b, n, hq, d = q.shape
    hkv = k.shape[2]
    g = hq // hkv
    nb = n // Q_BLOCK
    scale = 1.0 / math.sqrt(d)
    pad = ((0, 0), (Q_BLOCK, Q_BLOCK), (0, 0), (0, 0))
    kp = jnp.pad(k, pad).reshape(b, nb + 2, Q_BLOCK, hkv, d)
    vp = jnp.pad(v, pad).reshape(b, nb + 2, Q_BLOCK, hkv, d)

    def band(t):
        return jnp.concatenate([t[:, :-2], t[:, 1:-1], t[:, 2:]], axis=2)

    kb, vb = band(kp), band(vp)
    qb = q.reshape(b, nb, Q_BLOCK, hkv, g, d)
    s_loc = jnp.einsum('bnqhgd,bnjhd->bnhgqj', qb, kb).astype(jnp.float32) * scale
    blk = jnp.arange(nb)[:, None, None]
    qpos = blk * Q_BLOCK + jnp.arange(Q_BLOCK)[None, :, None]
    kpos = (blk - 1) * Q_BLOCK + jnp.arange(3 * Q_BLOCK)[None, None, :]
    valid = (kpos >= 0) & (kpos < n) & (jnp.abs(qpos - kpos) <= B_WINDOW)
    s_loc = jnp.where(valid[None, :, None, None], s_loc, NEG_INF)
    s_ctx = jnp.einsum('bnqhgd,blhd->bnhgql', qb, k_ctx).astype(jnp.float32) * scale
    p = softmax_sink(jnp.concatenate([s_loc, s_ctx], axis=-1), sink.reshape(hkv, g)[None, None, :, :, None, None])
    p = p.astype(v.dtype)
    p_loc, p_ctx = p[..., :3 * Q_BLOCK], p[..., 3 * Q_BLOCK:]
    o = (jnp.einsum('bnhgqj,bnjhd->bnqhgd', p_loc, vb)
         + jnp.einsum('bnhgql,blhd->bnqhgd', p_ctx, v_ctx))
    return o.reshape(b, n, hq, d)


def neighbourhood_attention(q, k, v, k_ctx, v_ctx, rpb):
    b, n, h, d = q.shape
    rows = n // GRID_W
    wh = min(NA_WIN_H, rows)
    ww = NA_WIN_W
    scale = 1.0 / math.sqrt(d)
    qg = q.reshape(b, rows, GRID_W, h, d)
    kg = k.reshape(b, rows, GRID_W, h, d)
    vg = v.reshape(b, rows, GRID_W, h, d)
    cols = jnp.arange(GRID_W)
    cstart = jnp.clip(cols - ww // 2, 0, GRID_W - ww)
    col_idx = cstart[:, None] + jnp.arange(ww)[None, :]
    col_off = col_idx - cols[:, None] + (NA_WIN_W - 1)
    bias_cols = rpb[:, :, col_off]

    def one_row(r):
        rs = jnp.clip(r - wh // 2, 0, rows - wh)
        q_r = lax.dynamic_index_in_dim(qg, r, axis=1, keepdims=False)
        k_r = lax.dynamic_slice_in_dim(kg, rs, wh, axis=1)[:, :, col_idx]
        v_r = lax.dynamic_slice_in_dim(vg, rs, wh, axis=1)[:, :, col_idx]
        row_off = rs + jnp.arange(wh) - r + (NA_WIN_H - 1)
        bias = bias_cols[:, row_off].transpose(0, 2, 1, 3)
        s_nb = jnp.einsum('bchd,bwcxhd->bhcwx', q_r, k_r).astype(jnp.float32) * scale + bias[None].astype(jnp.float32)
        s_nb = s_nb.reshape(b, h, GRID_W, wh * ww)
        s_ctx = jnp.einsum('bchd,blhd->bhcl', q_r, k_ctx).astype(jnp.float32) * scale
        p = jax.nn.softmax(jnp.concatenate([s_nb, s_ctx], axis=-1), axis=-1).astype(v.dtype)
        p_nb = p[..., :wh * ww].reshape(b, h, GRID_W, wh, ww)
        p_ctx = p[..., wh * ww:]
        return (jnp.einsum('bhcwx,bwcxhd->bchd', p_nb, v_r)
                + jnp.einsum('bhcl,blhd->bchd', p_ctx, v_ctx))

    out = lax.map(one_row, jnp.arange(rows))
    return out.transpose(1, 0, 2, 3, 4).reshape(b, n, h, d)


def even_mix_context(h, w_in, w_out, sink):
    b, n, _ = h.shape
    qa, ka, va, qb, kb, vb = split_heads(h @ w_in, EVEN_HEADS)
    oa = block_attention(qa, ka, va)
    ob = block_attention(qb, kb, vb, sink)
    out = jnp.concatenate([oa.reshape(b, n, -1), ob.reshape(b, n, -1)], axis=-1) @ w_out
    return out, ka, va, kb, vb


def even_mix_latent(h, ka_ctx, va_ctx, kb_ctx, vb_ctx, w_in, w_out, rpb, sink):
    b, n, _ = h.shape
    qa, ka, va, qb, kb, vb = split_heads(h @ w_in, EVEN_HEADS)
    oa = neighbourhood_attention(qa, ka, va, ka_ctx, va_ctx, rpb)
    ob = window_attention(axial_rope(qb), axial_rope(kb), vb, kb_ctx, vb_ctx, sink)
    return jnp.concatenate([oa.reshape(b, n, -1), ob.reshape(b, n, -1)], axis=-1) @ w_out


def odd_mix_context(h, w_in, w_out, q_norm, k_norm):
    b, n, _ = h.shape
    q, k, v = split_heads(h @ w_in, ODD_HEADS)
    q = rms_norm(q, q_norm)
    k = rms_norm(k, k_norm)
    o = block_attention(q, k, v)
    return o.reshape(b, n, -1) @ w_out, k, v


def odd_mix_latent(h, k_ctx, v_ctx, w_in, w_out, q_norm, k_norm):
    b, n, _ = h.shape
    q, k, v = split_heads(h @ w_in, ODD_HEADS)
    q = axial_rope(rms_norm(q, q_norm))
    k = axial_rope(rms_norm(k, k_norm))
    o = block_attention(q, jnp.concatenate([k_ctx, k], axis=1), jnp.concatenate([v_ctx, v], axis=1))
    return o.reshape(b, n, -1) @ w_out


def swiglu(h, w_gate_up, w_down):
    gate, up = jnp.split(h @ w_gate_up, 2, axis=-1)
    return (jax.nn.silu(gate) * up) @ w_down


def setup_inputs(seed: int = 0) -> dict:
    key = jax.random.key(seed)
    ks = jax.random.split(key, 26)

    def nrm(k, shape, s):
        return jax.random.normal(k, shape, jnp.float32) * s

    D = D_MODEL
    return {
        'x_prompt': nrm(ks[0], (BATCH, SEQ, D), 1.0),
        'x_sample': nrm(ks[1], (DEC_BATCH, DEC_SEQ, D), 1.0),
        'cache_a_k': nrm(ks[2], (DEC_BATCH, N_EVEN, PAST_LEN, A_HEADS, HEAD_DIM), 1.0),
        'cache_a_v': nrm(ks[3], (DEC_BATCH, N_EVEN, PAST_LEN, A_HEADS, HEAD_DIM), 1.0),
        'cache_b_k': nrm(ks[4], (DEC_BATCH, N_EVEN, PAST_LEN, B_KV_HEADS, HEAD_DIM), 1.0),
        'cache_b_v': nrm(ks[5], (DEC_BATCH, N_EVEN, PAST_LEN, B_KV_HEADS, HEAD_DIM), 1.0),
        'cache_c_k': nrm(ks[6], (DEC_BATCH, N_ODD, PAST_LEN, C_KV_HEADS, HEAD_DIM), 1.0),
        'cache_c_v': nrm(ks[7], (DEC_BATCH, N_ODD, PAST_LEN, C_KV_HEADS, HEAD_DIM), 1.0),
        'c': nrm(ks[8], (DEC_BATCH, D), 1.0),
        'c_ctx': nrm(ks[9], (D,), 1.0),
        'norm_gain': 1.0 + nrm(ks[10], (DEPTH, 2, D), 0.02),
        'w_mod': nrm(ks[11], (DEPTH, D, 6 * D), D ** -0.5),
        'b_mod': nrm(ks[12], (DEPTH, 6 * D), 0.02),
        'w_in_even': nrm(ks[13], (N_EVEN, D, EVEN_IN), D ** -0.5),
        'w_out_even': nrm(ks[14], (N_EVEN, EVEN_OUT, D), EVEN_OUT ** -0.5),
        'rpb_a': nrm(ks[15], (N_EVEN, A_HEADS, 2 * NA_WIN_H - 1, 2 * NA_WIN_W - 1), 0.1),
        'sink_b': nrm(ks[16], (N_EVEN, B_HEADS), 0.5),
        'w_in_odd': nrm(ks[17], (N_ODD, D, ODD_IN), D ** -0.5),
        'w_out_odd': nrm(ks[18], (N_ODD, ODD_OUT, D), ODD_OUT ** -0.5),
        'q_norm_c': 1.0 + nrm(ks[19], (N_ODD, HEAD_DIM), 0.02),
        'k_norm_c': 1.0 + nrm(ks[20], (N_ODD, HEAD_DIM), 0.02),
        'w_gate_up': nrm(ks[21], (DEPTH, D, 2 * D_FF), D ** -0.5),
        'w_down': nrm(ks[22], (DEPTH, D_FF, D), D_FF ** -0.5),
        'final_gain': 1.0 + nrm(ks[23], (D,), 0.02),
    }


def reference(x_prompt, x_sample, cache_a_k, cache_a_v, cache_b_k, cache_b_v, cache_c_k, cache_c_v,
              c, c_ctx, norm_gain, w_mod, b_mod, w_in_even, w_out_even, rpb_a, sink_b,
              w_in_odd, w_out_odd, q_norm_c, k_norm_c, w_gate_up, w_down, final_gain):
    ctx = x_prompt
    lat = x_sample
    a_k, a_v, b_k, b_v, c_k, c_v = [], [], [], [], [], []
    for l in range(DEPTH):
        csh1, csc1, cg1, csh2, csc2, cg2 = modulation(c_ctx[None, :], w_mod[l], b_mod[l])
        lsh1, lsc1, lg1, lsh2, lsc2, lg2 = modulation(c, w_mod[l], b_mod[l])
        hc = adaln(ctx, norm_gain[l, 0], csh1, csc1)
        hl = adaln(lat, norm_gain[l, 0], lsh1, lsc1)
        if l % 2 == 0:
            e = l // 2
            mix_c, ka, va, kb, vb = even_mix_context(hc, w_in_even[e], w_out_even[e], sink_b[e])
            mix_l = even_mix_latent(hl, cache_a_k[:, e], cache_a_v[:, e], cache_b_k[:, e], cache_b_v[:, e],
                                    w_in_even[e], w_out_even[e], rpb_a[e], sink_b[e])
            a_k.append(ka)
            a_v.append(va)
            b_k.append(kb)
            b_v.append(vb)
        else:
            o = l // 2
            mix_c, kc, vc = odd_mix_context(hc, w_in_odd[o], w_out_odd[o], q_norm_c[o], k_norm_c[o])
            mix_l = odd_mix_latent(hl, cache_c_k[:, o], cache_c_v[:, o], w_in_odd[o], w_out_odd[o],
                                   q_norm_c[o], k_norm_c[o])
            c_k.append(kc)
            c_v.append(vc)
        ctx = ctx + cg1 * mix_c
        lat = lat + lg1 * mix_l
        ctx = ctx + cg2 * swiglu(adaln(ctx, norm_gain[l, 1], csh2, csc2), w_gate_up[l], w_down[l])
        lat = lat + lg2 * swiglu(adaln(lat, norm_gain[l, 1], lsh2, lsc2), w_gate_up[l], w_down[l])
    y_prompt = rms_norm(ctx, final_gain)
    y_sample = rms_norm(lat, final_gain)
    state_a_k = jnp.stack(a_k, axis=1)
    state_a_v = jnp.stack(a_v, axis=1)
    state_b_k = jnp.stack(b_k, axis=1)
    state_b_v = jnp.stack(b_v, axis=1)
    state_c_k = jnp.stack(c_k, axis=1)
    state_c_v = jnp.stack(c_v, axis=1)
    return (y_prompt, y_sample, state_a_k, state_a_v, state_b_k, state_b_v, state_c_k, state_c_v)
```

```python
import contextlib
import numpy as np
import concourse.bass as bass
import concourse.mybir as mybir
from concourse.bass_utils import run_bass_kernel_spmd

F32 = mybir.dt.float32
BF16 = mybir.dt.bfloat16
AF = mybir.ActivationFunctionType
ALU = mybir.AluOpType
AX = mybir.AxisListType

ENGS = ("pe", "act", "dve", "pool", "sp")
DMA_RING = 8
NWR = 6
NDUM_CTX = 2
NDUM_LAT = 0
STRICT_SAME_ENGINE = True
NCORES = 8
D = 1024
TOK = 1024
DFF = 2816
FC = 22
EPS = 1e-6
NEG = -30000.0


class Op:
    __slots__ = ("eng", "fn", "deps", "dma", "sig", "sem", "val", "idx", "waits", "clock")

    def __init__(self, eng, fn, dma):
        self.eng = eng
        self.fn = fn
        self.dma = dma
        self.deps = set()
        self.sig = False
        self.sem = None
        self.val = 0
        self.waits = []
        self.clock = None


class Tracker:
    def __init__(self, nc):
        self.nc = nc
        self.ops = []
        self.res = {}
        self.gran = {}
        self.psum = set()
        self.dma_ops = {"sp": [], "pool": [], "act": []}

    def _slots(self, ap):
        if ap is None or not hasattr(ap, "tensor") or not hasattr(ap, "ap"):
            return ()
        sp = str(ap.space).upper()
        if "DRAM" in sp or "HBM" in sp:
            return ()
        t = ap.tensor
        pstep = 1
        for s in list(t.shape)[1:]:
            pstep *= int(s)
        dims = ap.ap
        col0 = int(ap.offset) % pstep
        ext = 1
        for (st, cnt) in dims[1:]:
            ext += (int(cnt) - 1) * abs(int(st))
        g = self.gran.get(t.name, 256)
        return [(t.name, s) for s in range(col0 // g, (col0 + ext - 1) // g + 1)]

    def add(self, eng, fn, reads=(), writes=(), dma=False):
        op = Op(eng, fn, dma)
        idx = len(self.ops)
        op.idx = idx
        deps = set()
        rs = []
        for a in reads:
            rs.extend(self._slots(a))
        ws = []
        for a in writes:
            ws.extend(self._slots(a))
        prs = [k for k in rs if k[0] in self.psum]
        if prs:
            ws = ws + [k for k in prs if k not in ws]
        for k in rs:
            st = self.res.get(k)
            if st is not None and st[0] is not None:
                deps.add(st[0])
        for k in ws:
            st = self.res.get(k)
            if st is not None:
                if st[0] is not None:
                    deps.add(st[0])
                deps.update(st[1].values())
                deps.update(st[2])
        keep = set()
        rset = set(rs)
        for d in deps:
            p = self.ops[d]
            if p.dma or dma:
                keep.add(d)
                continue
            if p.eng == eng:
                if eng == "pe":
                    continue
                if STRICT_SAME_ENGINE:
                    keep.add(d)
                    continue
                raw = False
                for k in rset:
                    st = self.res.get(k)
                    if st is not None and st[0] == d:
                        raw = True
                        break
                if raw:
                    keep.add(d)
                continue
            keep.add(d)
        if dma:
            q = self.dma_ops[eng]
            if len(q) >= DMA_RING:
                keep.add(q[len(q) - DMA_RING])
            q.append(idx)
        op.deps = keep
        for d in keep:
            self.ops[d].sig = True
        for k in ws:
            self.res[k] = [idx, {}, []]
        for k in rs:
            st = self.res.get(k)
            if st is None:
                st = [None, {}, []]
                self.res[k] = st
            if st[0] == idx:
                continue
            if dma:
                st[2].append(idx)
            else:
                st[1][eng] = idx
        self.ops.append(op)
        return idx

    def emit(self):
        nc = self.nc
        es = contextlib.ExitStack()
        sems = {}
        for e in ("pe", "act", "dve", "pool"):
            sems[e] = es.enter_context(nc.semaphore("s_" + e))
        for q in ("sp", "pool", "act"):
            if self.dma_ops[q]:
                for i in range(DMA_RING):
                    sems[(q, i)] = es.enter_context(nc.semaphore("d_%s%d" % (q, i)))
        cnt = {e: 0 for e in ("pe", "act", "dve", "pool")}
        dcnt = {"sp": 0, "pool": 0, "act": 0}
        for op in self.ops:
            if op.dma:
                i = dcnt[op.eng]
                dcnt[op.eng] += 1
                op.sem = (op.eng, i % DMA_RING)
                op.val = 16 * (i // DMA_RING + 1)
                op.sig = True
            elif op.sig:
                cnt[op.eng] += 1
                op.sem = op.eng
                op.val = cnt[op.eng]
        eclock = {e: {} for e in ENGS}
        for op in self.ops:
            ec = eclock[op.eng]
            sel = []
            tmp = dict(ec)
            for d in sorted(op.deps, reverse=True):
                p = self.ops[d]
                if tmp.get(p.sem, 0) >= p.val:
                    continue
                sel.append(p)
                for k, v in p.clock.items():
                    if tmp.get(k, 0) < v:
                        tmp[k] = v
            for q in list(sel):
                oth = dict(ec)
                for p in sel:
                    if p is q:
                        continue
                    for k, v in p.clock.items():
                        if oth.get(k, 0) < v:
                            oth[k] = v
                if oth.get(q.sem, 0) >= q.val:
                    sel.remove(q)
            waits = {}
            for p in sel:
                if waits.get(p.sem, 0) < p.val:
                    waits[p.sem] = p.val
                for k, v in p.clock.items():
                    if ec.get(k, 0) < v:
                        ec[k] = v
            op.waits = list(waits.items())
            c = dict(ec)
            if op.sig:
                c[op.sem] = max(c.get(op.sem, 0), op.val)
            op.clock = c
        streams = {e: [o for o in self.ops if o.eng == e] for e in ENGS}
        final = []
        for q in ("sp", "pool", "act"):
            n = dcnt[q]
            for r in range(DMA_RING):
                m = (n - r + DMA_RING - 1) // DMA_RING if n > r else 0
                if m > 0:
                    final.append(((q, r), 16 * m))
        self.nwaits = sum(len(o.waits) for o in self.ops)
        with es:
            with nc.Block() as block:
                def mk(ename):
                    def body(eng):
                        for o in streams[ename]:
                            for (s, v) in o.waits:
                                eng.wait_ge(sems[s], v)
                            ins = o.fn(eng)
                            if o.sig:
                                ins.then_inc(sems[o.sem], 16 if o.dma else 1)
                        if ename == "sp":
                            for (s, v) in final:
                                eng.wait_ge(sems[s], v)
                    return body
                block.tensor(mk("pe"))
                block.scalar(mk("act"))
                block.vector(mk("dve"))
                block.gpsimd(mk("pool"))
                block.sync(mk("sp"))


def _rs(r):
    return min(max(r - 4, 0), 8)


def _cs(c):
    return min(max(c - 8, 0), 48)


def nbr_exp_segments(qc, mp):
    r0 = 8 * qc
    flags = [tuple(1 if _rs(r0 + i) <= 2 * mp + j <= _rs(r0 + i) + 7 else 0 for j in range(2)) for i in range(8)]
    if not any(f != (0, 0) for f in flags):
        return None
    kind = {(1, 1): 0, (1, 0): 1, (0, 1): 2, (0, 0): 3}
    segs = []
    i = 0
    while i < 8:
        j = i
        while j + 1 < 8 and flags[j + 1] == flags[i]:
            j += 1
        segs.append((i * 64, (j + 1) * 64, kind[flags[i]]))
        i = j + 1
    return segs


def host_consts():
    c = {}
    c["idn"] = np.eye(128, dtype=np.float32)
    cb = np.zeros((128, 640), np.float32)
    cb[:, 0:128] = np.eye(128)
    cb[:, 128:256] = 1.0
    for p in range(128):
        cb[p, 256 + (p // 64) * 64: 256 + (p // 64) * 64 + 64] = 1.0
    for m in range(128):
        d = m % 64
        partner = d + 16 if (d % 32) < 16 else d - 16
        cb[(m // 64) * 64 + partner, 384 + m] = 1.0
    cb[0, 512 + 64:512 + 128] = 1.0
    c["cbh"] = cb
    t = np.arange(1024)
    row = (t // 64).astype(np.float32)
    col = (t % 64).astype(np.float32)
    inv = (np.float32(10000.0) ** (-(np.arange(0, 32, 2, dtype=np.float32)) / np.float32(32))).astype(np.float32)
    C = np.zeros((128, 1024), np.float32)
    S = np.zeros((128, 1024), np.float32)
    for p in range(128):
        d = p % 64
        pos = row if d < 32 else col
        ang = (pos * inv[d % 16]).astype(np.float32)
        C[p] = np.cos(ang)
        S[p] = -np.sin(ang) if (d % 32) < 16 else np.sin(ang)
    c["ropeC"] = C
    c["ropeS"] = S
    wm = np.zeros((128, 2, 4, 128), np.float32)
    b = np.arange(128)[:, None]
    a = np.arange(128)[None, :]
    wm[:, 0] = np.where(b >= a, 1.0, 0.0)[:, None, :]
    wm[:, 1] = np.where(b <= a, 1.0, 0.0)[:, None, :]
    c["wmask"] = wm.reshape(128, 1024)
    nm = np.zeros((128, 22, 64), np.float32)
    for p in range(128):
        cp = p % 64
        for cc in range(64):
            if not (_cs(cc) <= cp <= _cs(cc) + 15):
                nm[p, :, cc] = NEG
    c["nbmask"] = nm.reshape(128, 1408)
    return c


def rpb_gather_index():
    p = np.arange(128)
    jp = (p // 64)[:, None, None]
    cp = (p % 64)[:, None, None]
    e = np.arange(22)[None, :, None]
    cc = np.arange(64)[None, None, :]
    dr = jp - (e - 10) + 0 * cc
    dc = cp - cc + 0 * e
    ir = np.clip(dr + 7, 0, 14)
    ic = np.clip(dc + 15, 0, 30)
    return ir, ic


def build_program(stop_after=None, dbg=()):
    nc = bass.Bass("TRN2", target_bir_lowering=False)
    T = Tracker(nc)
    es = contextlib.ExitStack()

    def din(name, shape):
        return nc.dram_tensor(name, list(shape), F32, kind="ExternalInput").ap()

    def dout(name, shape):
        return nc.dram_tensor(name, list(shape), F32, kind="ExternalOutput").ap()

    xg = [din("xg0", (TOK, D)), din("xg1", (TOK, D))]
    cak = din("cak", (512, 512)); cav = din("cav", (512, 512))
    cbk = din("cbk", (512, 128)); cbv = din("cbv", (512, 128))
    cck = din("cck", (512, 256)); ccv = din("ccv", (512, 256))
    vecs = din("vecs", (58, 128)); bmod = din("bmod", (96, 128))
    sink = din("sink", (1, 8)); kng = din("kng", (1, 64))
    wmod = din("wmod", (2, D, 6 * D))
    wfe = din("wfe", (D, 1664)); wie = din("wie", (D, 2304)); woe = din("woe", (D, D))
    wfo = din("wfo", (D, 1280)); wio = din("wio", (D, 1536)); woo = din("woo", (D, D))
    wgu = din("wgu", (2, D, 2 * DFF)); wdn = din("wdn", (2, DFF, D))
    rpbt = din("rpbt", (8, 128, 1408))
    idn = din("idn", (128, 128)); cbh = din("cbh", (128, 640))
    ropeC_h = din("ropeC", (128, 1024)); ropeS_h = din("ropeS", (128, 1024))
    wmask_h = din("wmask", (128, 1024)); nbmask_h = din("nbmask", (128, 1408))

    yg = [dout("yg0", (TOK, D)), dout("yg1", (TOK, D))]
    sak = dout("sak", (TOK, 512)); sav = dout("sav", (TOK, 512))
    sbk = dout("sbk", (TOK, 128)); sbv = dout("sbv", (TOK, 128))
    sck = dout("sck", (TOK, 256)); scv = dout("scv", (TOK, 256))

    def sb(name, shape, dt, gran=None):
        t = es.enter_context(nc.sbuf_tensor(name, list(shape), dt))
        if gran:
            T.gran[name] = gran
        return t

    def ps(name, shape, dt):
        t = es.enter_context(nc.psum_tensor(name, list(shape), dt))
        T.gran[name] = 1 << 20
        T.psum.add(name)
        return t

    xT = sb("xT", [128, 8, TOK], F32)
    HO = sb("HO", [128, 8, TOK], BF16)
    SCR = sb("SCR", [128, 24576], BF16)
    VC = sb("VC", [128, 4, 10, 128], BF16)
    KCT = sb("KCT", [128, 5, 512], BF16)
    CST = sb("CST", [128, 4, 256], BF16)
    PT = sb("PT", [128, 6, 512], BF16)
    BIAS = sb("BIAS", [128, 4, 1408], BF16, gran=1408)
    NBM = sb("NBM", [128, 1408], BF16)
    WR = [sb("WR%d" % i, [128, 2048], BF16, gran=1 << 20) for i in range(NWR)]
    STG = sb("STG", [128, 4, 1024], F32)
    idf = sb("idf", [128, 128], F32)
    CB = sb("CB", [128, 640], BF16)
    ROC = sb("ROC", [128, 1024], F32)
    ROS = sb("ROS", [128, 1024], F32)
    WM = sb("WM", [128, 2, 512], BF16)
    TMPF = sb("TMPF", [128, 4, 512], F32, gran=512)
    TMPB = sb("TMPB", [128, 4, 512], BF16, gran=512)
    RCP = sb("RCP", [128, 1, 512], F32, gran=512)
    stA = sb("stA", [96, 128], F32)
    stB = sb("stB", [58, 128], F32)
    bmodT = sb("bmodT", [128, 96], F32)
    vecT = sb("vecT", [128, 58], F32)
    sc2b = sb("sc2b", [128, 8, 2], BF16)
    MROW = sb("MROW", [2, 2, 256], F32, gran=256)
    MOD = sb("MOD", [128, 2, 6, 8, 2], F32)
    modr = sb("modr", [128, 2, 48, 2], F32)
    epsT = sb("epsT", [128, 1], F32)
    sinkT = sb("sinkT", [1, 8], F32)
    esinkA = sb("esinkA", [1, 8, 128], BF16)
    kgB = sb("kgB", [128, 64], F32)
    SM = sb("SM", [128, 64], F32)
    BCOL = sb("BCOL", [128, 3], F32)

    PS = [ps("ps%d" % i, [128, 512], F32) for i in range(7)]
    PSB = ps("psb", [128, 1024], BF16)

    ident_b = CB[:, 0:128]
    ones_b = CB[:, 128:256]
    blk_b = CB[:, 256:384]
    perm_b = CB[:, 384:512]
    erow_b = CB[0:1, 512:640]

    def aps(*xs):
        return [x for x in xs if hasattr(x, "tensor")]

    def dma(q, out, in_):
        T.add(q, lambda e: e.dma_start(out=out, in_=in_), reads=aps(in_), writes=aps(out), dma=True)

    def mm(out, lhsT, rhs, start, stop):
        T.add("pe", lambda e: e.matmul(out, lhsT=lhsT, rhs=rhs, start=start, stop=stop,
                                       skip_group_check=True),
              reads=[lhsT, rhs], writes=[out])

    def tr(out, in_, ident):
        T.add("pe", lambda e: e.transpose(out=out, in_=in_, identity=ident), reads=[in_, ident], writes=[out])

    def act(out, in_, func, bias=None, scale=None, eng="act"):
        kw = {}
        if bias is not None:
            kw["bias"] = bias
        if scale is not None:
            kw["scale"] = scale
        T.add("act", lambda e: e.activation(out=out, in_=in_, func=func, **kw),
              reads=aps(in_, bias, scale), writes=[out])

    def tt(out, in0, in1, op, eng="dve"):
        T.add(eng, lambda e: e.tensor_tensor(out=out, in0=in0, in1=in1, op=op), reads=[in0, in1], writes=[out])

    def ts(out, in0, s1, op0, s2=None, op1=None, eng="dve"):
        if op1 is None:
            T.add(eng, lambda e: e.tensor_scalar(out=out, in0=in0, scalar1=s1, scalar2=None, op0=op0),
                  reads=aps(in0, s1), writes=[out])
        else:
            T.add(eng, lambda e: e.tensor_scalar(out=out, in0=in0, scalar1=s1, scalar2=s2, op0=op0, op1=op1),
                  reads=aps(in0, s1, s2), writes=[out])

    def stt(out, in0, scalar, in1, op0, op1, eng="dve"):
        T.add(eng, lambda e: e.scalar_tensor_tensor(out=out, in0=in0, scalar=scalar, in1=in1, op0=op0, op1=op1),
              reads=aps(in0, scalar, in1), writes=[out])

    def cp(out, in_, eng="dve"):
        T.add(eng, lambda e: e.tensor_copy(out=out, in_=in_), reads=[in_], writes=[out])

    def recip(out, in_):
        T.add("dve", lambda e: e.reciprocal(out=out, in_=in_), reads=[in_], writes=[out])

    def recip_act(out, in_):
        act(out, in_, AF.Ln)
        act(out, out, AF.Exp, scale=-1.0)

    def memset(ap, v, eng="dve"):
        T.add(eng, lambda e: e.memset(ap, v), writes=[ap])

    rot = {}

    def nxt(key, n):
        v = rot.get(key, 0)
        rot[key] = v + 1
        return v % n

    def psum_lin():
        return PS[nxt("lin", 2)]

    def psum_aux():
        return PS[2 + nxt("aux", 2)]

    def psum_proj():
        return PS[(0, 1, 4, 5)[nxt("proj", 4)]]

    def psum_s():
        return PS[3 + nxt("s", 3)]

    def tmpf():
        return TMPF[:, nxt("tmpf", 4), :]

    def tmpb():
        return TMPB[:, nxt("tmpb", 4), :]

    wr_state = {"i": 0}

    def wload(src_ap, kc, ncol):
        slot = WR[wr_state["i"] % NWR]
        wr_state["i"] += 1
        view = slot[:, 0:kc * ncol].rearrange("p (k n) -> p k n", k=kc)
        dma("pool", view, src_ap)
        return view

    def wsrc(w2d, c0, c1):
        return w2d.rearrange("(k p) n -> p k n", p=128)[:, :, c0:c1]

    dma("sp", idf[:], idn)
    dma("pool", CB[:], cbh)
    dma("sp", ROC[:], ropeC_h)
    dma("sp", ROS[:], ropeS_h)
    dma("pool", WM[:].rearrange("p a b -> p (a b)"), wmask_h)
    dma("pool", NBM[:], nbmask_h)
    dma("sp", stA[:], bmod)
    dma("sp", stB[:], vecs)
    dma("sp", sinkT[:], sink)
    dma("sp", kgB[:], bass.AP(tensor=kng.tensor, offset=0, ap=[[0, 128], [1, 64]]))
    memset(epsT[:], EPS)
    memset(BCOL[:, 0:2], 0.0)
    memset(BCOL[64:128, 0:1], NEG)
    memset(BCOL[0:64, 1:2], NEG)
    memset(BCOL[:, 2:3], NEG)
    for t4 in range(4):
        memset(VC[:, t4, :, 64:128], 1.0)

    pA = psum_aux()
    tr(pA[:, 0:96], stA[:, :], idf[0:96, 0:96])
    cp(bmodT[:], pA[:, 0:96])
    pB = psum_aux()
    tr(pB[:, 0:58], stB[:, :], idf[0:58, 0:58])
    cp(vecT[:], pB[:, 0:58])
    act(sc2b[:, :, 0], vecT[:, 40:48], AF.Silu)
    act(sc2b[:, :, 1], vecT[:, 48:56], AF.Silu)
    act(sinkT[:], sinkT[:], AF.Exp)
    cp(esinkA[:], sinkT[:].unsqueeze(2).broadcast_to([1, 8, 128]))

    pM = PS[6]

    def mod_part1(l, nt):
        wv = wload(wsrc(wmod[l], nt * 256, (nt + 1) * 256), 8, 256)
        pr = PS[2]
        for kc in range(8):
            mm(pr[0:2, 0:256], sc2b[:, kc, :], wv[:, kc, :], start=(kc == 0), stop=(kc == 7))
        mrow = MROW[0:2, nt % 2, :]
        cp(mrow, pr[0:2, 0:256])

    def mod_part2(l, nt):
        mrow = MROW[0:2, nt % 2, :]
        for j in range(2):
            jj = l * 48 + nt * 2 + j
            tr(pM[:, jj * 2:jj * 2 + 2], mrow[:, j * 128:(j + 1) * 128], idf[0:2, 0:2])

    mod_state = {0: 0, 1: 0}

    def modulation_step(l):
        i = mod_state[l]
        if i > 24:
            return False
        if i < 24:
            mod_part1(l, i)
        if i >= 1:
            mod_part2(l, i - 1)
        mod_state[l] = i + 1
        return True

    def finalize_mod(l):
        tt(modr[:, l], pM[:, l * 96:(l + 1) * 96].rearrange("p (j r) -> p j r", r=2),
           bmodT[:, l * 48:(l + 1) * 48].unsqueeze(2).broadcast_to([128, 48, 2]), ALU.add)
        mr = modr[:, l].rearrange("p (s k) r -> p s k r", s=6)
        for half in range(2):
            g = vecT[:, (l * 2 + half) * 8:(l * 2 + half) * 8 + 8]
            sh, scl, gt = mr[:, 3 * half + 0], mr[:, 3 * half + 1], mr[:, 3 * half + 2]
            ts(MOD[:, l, 3 * half + 0], scl, 1.0, ALU.add)
            tt(MOD[:, l, 3 * half + 0], MOD[:, l, 3 * half + 0], g.unsqueeze(2).broadcast_to([128, 8, 2]), ALU.mult)
            cp(MOD[:, l, 3 * half + 1], sh)
            cp(MOD[:, l, 3 * half + 2], gt)

    def prologue_mod0():
        while modulation_step(0):
            pass
        finalize_mod(0)

    dbg_out = []

    def dump(name, ap, shape):
        if name in dbg:
            o = nc.dram_tensor("dbg_" + name, list(shape), ap.dtype, kind="ExternalOutput").ap()
            dma("sp", o, ap)

    def load_x(g):
        for t in range(8):
            st = STG[:, nxt("stg", 4), :]
            dma("sp", st, xg[g][t * 128:(t + 1) * 128, :])
            for half in range(2):
                p = psum_lin()
                for q in range(4):
                    kc = half * 4 + q
                    tr(p[:, q * 128:(q + 1) * 128], st[:, kc * 128:(kc + 1) * 128], idf[:])
                dst = xT[:, half * 4:half * 4 + 4, t * 128:(t + 1) * 128]
                src = p[:, :].rearrange("p (a b) -> p a b", a=4)
                if half == 0:
                    act(dst, src, AF.Identity)
                else:
                    cp(dst, src)

    ST = (PS[2], PS[3])
    stats = {"ready": False, "pend": []}

    def stats_push(m, tt_i):
        sq = tmpb()
        act(sq, xT[:, m, tt_i * 512:(tt_i + 1) * 512], AF.Square)
        stats["pend"].append((m, tt_i, sq))
        if len(stats["pend"]) > 1:
            stats_pop()

    def stats_pop():
        m, tt_i, sq = stats["pend"].pop(0)
        mm(ST[tt_i][:, :], ones_b, sq, start=(m == 0), stop=(m == 7))

    def stats_flush():
        while stats["pend"]:
            stats_pop()
        stats["ready"] = True

    def rstd_banks():
        if not stats["ready"]:
            for m in range(8):
                for tt_i in range(2):
                    stats_push(m, tt_i)
            stats_flush()
        for tt_i in range(2):
            act(ST[tt_i][:, :], ST[tt_i][:, :], AF.Ln, bias=epsT[:, 0:1], scale=1.0 / D)
            act(ST[tt_i][:, :], ST[tt_i][:, :], AF.Exp, scale=-0.5)
        stats["ready"] = False

    def adaln(l, half, r_i):
        rstd_banks()
        for tt_i in range(2):
            for kc in range(8):
                tmp = tmpf()
                tt(tmp, xT[:, kc, tt_i * 512:(tt_i + 1) * 512], ST[tt_i][:, :], ALU.mult)
                act(HO[:, kc, tt_i * 512:(tt_i + 1) * 512], tmp, AF.Identity,
                    scale=MOD[:, l, 3 * half, kc, r_i:r_i + 1], bias=MOD[:, l, 3 * half + 1, kc, r_i:r_i + 1])

    def final_norm(g):
        rstd_banks()
        for tt_i in range(2):
            for kc in range(8):
                sl = xT[:, kc, tt_i * 512:(tt_i + 1) * 512]
                stt(sl, sl, vecT[:, 32 + kc:33 + kc], ST[tt_i][:, :], ALU.mult, ALU.mult)
        for t in range(8):
            st = STG[:, nxt("stg", 4), :]
            for half in range(2):
                p = psum_lin()
                for q in range(4):
                    kc = half * 4 + q
                    tr(p[:, q * 128:(q + 1) * 128], xT[:, kc, t * 128:(t + 1) * 128], idf[:])
                if half == 0:
                    act(st[:, 0:512], p[:, :], AF.Identity)
                else:
                    cp(st[:, 512:1024], p[:, :])
            dma("sp", yg[g][t * 128:(t + 1) * 128, :], st)

    def linear_fm(wview, ncols_chunks, src, kcn, post, palloc=None):
        for j in range(ncols_chunks):
            for tt_i in range(2):
                p = (palloc or psum_lin)()
                for kc in range(kcn):
                    mm(p[:, :], wview[:, kc, j * 128:(j + 1) * 128], src[:, kc, tt_i * 512:(tt_i + 1) * 512],
                       start=(kc == 0), stop=(kc == kcn - 1))
                post(j, tt_i, p)

    def resid_update(l, w, r_i):
        def post(m, tt_i, p):
            sl = xT[:, m, tt_i * 512:(tt_i + 1) * 512]
            stt(sl, p[:, :], MOD[:, l, w, m, r_i:r_i + 1], sl, ALU.mult, ALU.add)
            stats_push(m, tt_i)
        return post

    def ffn(l, r_i):
        actT = SCR[:, 0:FC * TOK].rearrange("p (k n) -> p k n", k=FC)
        for jj in range(11):
            wg = wload(wsrc(wgu[l], jj * 256, jj * 256 + 256), 8, 256)
            wu = wload(wsrc(wgu[l], DFF + jj * 256, DFF + jj * 256 + 256), 8, 256)
            for j in range(2):
                jg = jj * 2 + j
                for tt_i in range(2):
                    pg = psum_lin()
                    pu = psum_aux()
                    for kc in range(8):
                        mm(pg[:, :], wg[:, kc, j * 128:(j + 1) * 128], HO[:, kc, tt_i * 512:(tt_i + 1) * 512],
                           start=(kc == 0), stop=(kc == 7))
                    for kc in range(8):
                        mm(pu[:, :], wu[:, kc, j * 128:(j + 1) * 128], HO[:, kc, tt_i * 512:(tt_i + 1) * 512],
                           start=(kc == 0), stop=(kc == 7))
                    sg = tmpf()
                    act(sg, pg[:, :], AF.Silu)
                    tt(actT[:, jg, tt_i * 512:(tt_i + 1) * 512], sg, pu[:, :], ALU.mult)
        wdv = wdn[l].rearrange("(k p) n -> p k n", p=128)
        for m in range(8):
            wd0 = wload(wdv[:, 0:11, m * 128:(m + 1) * 128], 11, 128)
            wd1 = wload(wdv[:, 11:22, m * 128:(m + 1) * 128], 11, 128)
            for tt_i in range(2):
                p = psum_lin()
                for k in range(FC):
                    wd = wd0 if k < 11 else wd1
                    mm(p[:, :], wd[:, k % 11, :], actT[:, k, tt_i * 512:(tt_i + 1) * 512], start=(k == 0), stop=(k == FC - 1))
                resid_update(l, 5, r_i)(m, tt_i, p)
        stats_flush()

    qk_pipe = []

    def qk_stage_a(it):
        p = it["p"]
        if it["gain"] is not None:
            it["sq"] = tmpb()
            act(it["sq"], p, AF.Square)
        elif it["rope"]:
            it["cur"] = tmpb()
            act(it["cur"], p, AF.Identity)
        else:
            act(it["dst"], p, AF.Identity)
            it["done"] = True

    def qk_stage_b(it):
        if it.get("done"):
            return
        if it["gain"] is not None:
            pq = psum_aux()
            mm(pq[:, :], blk_b, it["sq"], start=True, stop=True)
            sd = tmpf()
            act(sd, pq[:, :], AF.Ln, bias=epsT[:, 0:1], scale=1.0 / 64)
            act(sd, sd, AF.Exp, scale=-0.5)
            tgt = tmpb() if it["rope"] else it["dst"]
            stt(tgt, it["p"], it["gain"], sd, ALU.mult, ALU.mult)
            it["cur"] = tgt
            if not it["rope"]:
                it["done"] = True

    def qk_stage_c(it):
        if it.get("done"):
            return
        tt_i = it["tt"]
        pr = psum_aux()
        mm(pr[:, :], perm_b, it["cur"], start=True, stop=True)
        t1 = tmpf()
        tt(t1, it["cur"], ROC[:, tt_i * 512:(tt_i + 1) * 512], ALU.mult)
        t2 = tmpf()
        tt(t2, pr[:, :], ROS[:, tt_i * 512:(tt_i + 1) * 512], ALU.mult)
        tt(it["dst"], t1, t2, ALU.add)
        it["done"] = True

    def qk_post(p, dst, tt_i, norm_gain_col, rope):
        it = dict(p=p, dst=dst, tt=tt_i, gain=norm_gain_col, rope=rope)
        qk_pipe.append(it)
        qk_stage_a(it)
        if len(qk_pipe) >= 2:
            qk_stage_b(qk_pipe[-2])
        if len(qk_pipe) >= 3:
            qk_stage_c(qk_pipe[-3])

    def qk_flush():
        n = len(qk_pipe)
        if n >= 1:
            qk_stage_b(qk_pipe[-1])
        if n >= 2:
            qk_stage_c(qk_pipe[-2])
        if n >= 1:
            qk_stage_c(qk_pipe[-1])
        del qk_pipe[:]

    def attend(units, between=None, sbanks=(3, 4, 5), lag=2, obanks=(0, 1), dummy=None, paired=False):
        flat = []
        for ui, u in enumerate(units):
            for ti, tl in enumerate(u["tiles"]):
                flat.append((ui, ti))
        obank = {}
        pend = []

        def do_pv(item):
            ui, ti, ptile = item
            u = units[ui]
            tl = u["tiles"][ti]
            N = u["N"]
            if ti == 0:
                obank[ui] = PS[obanks[nxt("o", len(obanks))]]
            po = obank[ui]
            first = (ti == 0)
            for (c0, c1, pl, ph, vl) in tl["pv"]:
                mm(po[:, c0:c1], vl[pl:ph, :], ptile[pl:ph, c0:c1], start=first, stop=True)
                first = False
            if ti == len(u["tiles"]) - 1:
                for (c0, c1, lh, rh) in u.get("extra", []):
                    mm(po[:, c0:c1], lh, rh, start=False, stop=True)
                rc = RCP[:, 0, :]
                if u.get("recip") == "act":
                    recip_act(rc[0:64, 0:N], po[64:128, 0:N])
                else:
                    recip(rc[0:64, 0:N], po[64:128, 0:N])
                for (c0, c1, dst) in u["outs"]:
                    tt(dst, po[0:64, c0:c1], rc[0:64, c0:c1], ALU.mult)

        groups = []
        if paired:
            for i in range(0, len(units), 2):
                na, nb = len(units[i]["tiles"]), len(units[i + 1]["tiles"])
                assert na == nb
                for ti in range(na):
                    groups.append([(i, ti), (i + 1, ti)])
        else:
            groups = [[it] for it in flat]

        for grp in groups:
            infos = []
            slists = []
            for (ui, ti) in grp:
                u = units[ui]
                if ti == 0 and "prep" in u:
                    u["prep"]()
                tl = u["tiles"][ti]
                pS = PS[sbanks[nxt("s", len(sbanks))]]
                infos.append((ui, ti, pS))
                ops = [(pS, 0, u["N"], lh, rh) for (lh, rh) in tl.get("pre", [])]
                ops += [(pS, c0, c1, lh, rh) for (c0, c1, lh, rh) in tl["s"]]
                ops += [(pS, 0, u["N"], lh, rh) for (lh, rh) in tl.get("post", [])]
                slists.append((tl["kt"], ops))
            started = set()
            for k in range(max(len(o) for (_, o) in slists)):
                for (kt, ops) in slists:
                    if k < len(ops):
                        pS, c0, c1, lh, rh = ops[k]
                        mm(pS[0:kt, c0:c1], lh, rh, start=(pS.name not in started), stop=True)
                        started.add(pS.name)
            for (ui, ti, pS) in infos:
                u = units[ui]
                tl = u["tiles"][ti]
                N = u["N"]
                kt = tl["kt"]
                ptile = PT[:, nxt("pt", 6), :]
                if "segs" in tl:
                    for (c0, c1, kind) in tl["segs"]:
                        if kind == 0:
                            act(ptile[0:kt, c0:c1], pS[0:kt, c0:c1], AF.Exp, scale=0.125)
                        else:
                            act(ptile[0:kt, c0:c1], pS[0:kt, c0:c1], AF.Exp, scale=0.125, bias=BCOL[:, kind - 1:kind])
                else:
                    act(ptile[0:kt, 0:N], pS[0:kt, 0:N], AF.Exp, scale=0.125)
                if "mul" in tl:
                    mul_ap, mul_eng = tl["mul"]
                    m0, m1 = tl.get("cr", (0, N))
                    tt(ptile[0:kt, m0:m1], ptile[0:kt, m0:m1], mul_ap, ALU.mult, eng=mul_eng)
                pend.append((ui, ti, ptile))
            while len(pend) > lag:
                do_pv(pend.pop(0))
            for (ui, ti, pS) in infos:
                if between is not None and ti == len(units[ui]["tiles"]) - 1:
                    between(ui)
        while pend:
            do_pv(pend.pop(0))

    def attention(l, g):
        nch = 13 if l == 0 else 10
        nkv = 10 if l == 0 else 4
        QKT = SCR[:, 0:nch * TOK].rearrange("p (c n) -> p c n", c=nch)
        VA = SCR[:, 14336:14336 + 8 * nkv * 128].rearrange("p (t h d) -> p t h d", t=8, h=nkv)
        for t8 in range(8):
            memset(VA[:, t8, :, 64:128], 1.0)
        wf = wfe if l == 0 else wfo
        wi = wie if l == 0 else wio
        wo = woe if l == 0 else woo
        ncolf = nch * 128

        def bias_prep(h):
            bslot = BIAS[:, h % 4, :]
            dma("pool", bslot, rpbt[h])
            tt(bslot, bslot, NBM[:], ALU.add)
            act(bslot, bslot, AF.Exp)

        if g == 1:
            if l == 0:
                bias_prep(0)
                bias_prep(1)
                for hc in range(2):
                    dma("pool", CST[:], cak.rearrange("(t p) f -> p t f", p=128)[:, :, hc * 256:(hc + 1) * 256])
                    for c2 in range(2):
                        for t4 in range(4):
                            tr(PSB[:, t4 * 128:(t4 + 1) * 128], CST[:, t4, c2 * 128:(c2 + 1) * 128], ident_b)
                        cp(KCT[:, hc * 2 + c2, :], PSB[:, 0:512])
                dma("pool", CST[:, :, 0:128], cbk.rearrange("(t p) f -> p t f", p=128))
                for t4 in range(4):
                    tr(PSB[:, 512 + t4 * 128:512 + (t4 + 1) * 128], CST[:, t4, 0:128], ident_b)
                cp(KCT[:, 4, :], PSB[:, 512:1024])
                for t4 in range(4):
                    dma("pool", VC[:, t4, 0:8, 0:64], cav[t4 * 128:(t4 + 1) * 128, :].rearrange("p (h d) -> p h d", d=64))
                    dma("pool", VC[:, t4, 8:10, 0:64], cbv[t4 * 128:(t4 + 1) * 128, :].rearrange("p (h d) -> p h d", d=64))
            else:
                dma("pool", CST[:, :, 0:256], cck.rearrange("(t p) f -> p t f", p=128))
                for c in range(2):
                    for t4 in range(4):
                        tr(PSB[:, t4 * 128:(t4 + 1) * 128], CST[:, t4, c * 128:(c + 1) * 128], ident_b)
                    cp(KCT[:, c, :], PSB[:, 0:512])
                for t4 in range(4):
                    dma("pool", VC[:, t4, 0:4, 0:64], ccv[t4 * 128:(t4 + 1) * 128, :].rearrange("p (h d) -> p h d", d=64))

        def post_fm(c0):
            def post(j, tt_i, p):
                ch = c0 + j
                dst = QKT[:, ch, tt_i * 512:(tt_i + 1) * 512]
                if l == 0:
                    rope = (g == 1 and ch >= 8)
                    qk_post(p[:, :], dst, tt_i, None, rope)
                else:
                    gcol = vecT[:, 56:57] if ch < 8 else vecT[:, 57:58]
                    qk_post(p[:, :], dst, tt_i, gcol, g == 1)
            return post

        c0 = 0
        while c0 < nch:
            n = min(2, nch - c0)
            wv = wload(wsrc(wf, c0 * 128, (c0 + n) * 128), 8, n * 128)
            linear_fm(wv, n, HO, 8, post_fm(c0), palloc=psum_proj)
            c0 += n
        qk_flush()

        if stop_after == ("fm", l, g):
            return False

        def tokmaj(c0, ncol, post):
            wvs = []
            cc = 0
            while cc < ncol:
                n = min(256, ncol - cc)
                wvs.append((cc, n, wload(wsrc(wi, c0 + cc, c0 + cc + n), 8, n)))
                cc += n
            for t in range(8):
                p = psum_lin()
                for (cc, n, wv) in wvs:
                    for kc in range(8):
                        mm(p[:, cc:cc + n], HO[:, kc, t * 128:(t + 1) * 128], wv[:, kc, 0:n], start=(kc == 0 and cc == 0), stop=(kc == 7))
                post(t, p)

        if l == 0:
            if g == 1:
                tokmaj(1024, 512, lambda t, p: cp(VA[:, t, 0:8, 0:64], p[:, 0:512].rearrange("p (h d) -> p h d", d=64)))
                tokmaj(2176, 128, lambda t, p: cp(VA[:, t, 8:10, 0:64], p[:, 0:128].rearrange("p (h d) -> p h d", d=64)))
            else:
                def post_k(out_d, ncol):
                    def post(t, p):
                        st = STG[:, nxt("stg", 4), :]
                        act(st[:, 0:ncol], p[:, 0:ncol], AF.Identity)
                        dma("sp", out_d[t * 128:(t + 1) * 128, :], st[:, 0:ncol])
                    return post

                def post_v(out_d, ncol, h0, h1):
                    def post(t, p):
                        st = STG[:, nxt("stg", 4), :]
                        act(st[:, 0:ncol], p[:, 0:ncol], AF.Identity)
                        dma("sp", out_d[t * 128:(t + 1) * 128, :], st[:, 0:ncol])
                        cp(VA[:, t, h0:h1, 0:64], p[:, 0:ncol].rearrange("p (h d) -> p h d", d=64))
                    return post
                tokmaj(512, 512, post_k(sak, 512))
                tokmaj(1024, 512, post_v(sav, 512, 0, 8))
                tokmaj(2048, 128, post_k(sbk, 128))
                tokmaj(2176, 128, post_v(sbv, 128, 8, 10))
        else:
            if g == 1:
                tokmaj(1280, 256, lambda t, p: cp(VA[:, t, 0:4, 0:64], p[:, 0:256].rearrange("p (h d) -> p h d", d=64)))
            else:
                def post_kv(t, p):
                    st = STG[:, nxt("stg", 4), :]
                    act(st[:, 256:512], p[:, 256:512], AF.Identity)
                    dma("sp", scv[t * 128:(t + 1) * 128, :], st[:, 256:512])
                    cp(VA[:, t, 0:4, 0:64], p[:, 256:512].rearrange("p (h d) -> p h d", d=64))
                    kf = st[:, 0:256]
                    act(kf, p[:, 0:256], AF.Identity)
                    sq = st[:, 512:768]
                    tt(sq, kf, kf, ALU.mult)
                    ss = SM[:, 0:4]
                    T.add("dve", lambda e: e.tensor_reduce(out=ss, in_=sq.rearrange("p (h d) -> p h d", d=64),
                                                           axis=AX.X, op=ALU.add),
                          reads=[sq], writes=[ss])
                    sd = SM[:, 8:12]
                    act(sd, ss, AF.Sqrt, bias=epsT[:, 0:1], scale=1.0 / 64)
                    recip(sd, sd)
                    kn = st[:, 768:1024]
                    tt(kn.rearrange("p (h d) -> p h d", d=64), kf.rearrange("p (h d) -> p h d", d=64),
                       sd.unsqueeze(2).broadcast_to([128, 4, 64]), ALU.mult)
                    tt(kn.rearrange("p (h d) -> p h d", d=64), kn.rearrange("p (h d) -> p h d", d=64),
                       kgB[:].unsqueeze(1).broadcast_to([128, 4, 64]), ALU.mult)
                    dma("sp", sck[t * 128:(t + 1) * 128, :], kn)
                tokmaj(1024, 512, post_kv)

        if stop_after == ("proj", l, g):
            return False

        OT = HO
        units = []

        def qap(ch, half, t0, n):
            return QKT[half * 64:(half + 1) * 64, ch, t0:t0 + n]

        def otap(h, chunk0, t0, n):
            return OT[(h % 2) * 64:(h % 2 + 1) * 64, chunk0 + h // 2, t0:t0 + n]

        if l == 0:
            qpos_b = lambda h: (8 + h % 4, h // 4)
            kpos_b = lambda j: (12, j)
        else:
            qpos_c = lambda h: ((h // 8) * 4 + h % 4, (h // 4) % 2)
            kpos_c = lambda j: (8 + j // 2, j % 2)

        if g == 0:
            for s_ in range(4):
                t0 = s_ * 256

                def gqa_units(nkvh, qpos, kpos, vbase, obase, sink):
                    for kv in range(nkvh):
                        kch, khf = kpos(kv)
                        for pr in range(2):
                            hs = [kv * 4 + pr * 2, kv * 4 + pr * 2 + 1]
                            tiles = []
                            for kt_i in range(2):
                                k0 = t0 + kt_i * 128
                                sl = []
                                for i, h in enumerate(hs):
                                    qch, qhf = qpos(h)
                                    sl.append((i * 256, (i + 1) * 256, qap(kch, khf, k0, 128), qap(qch, qhf, t0, 256)))
                                tiles.append(dict(kt=128, s=sl, pv=[(0, 512, 0, 128, VA[:, s_ * 2 + kt_i, vbase + kv, :])]))
                            u = dict(N=512, tiles=tiles,
                                     outs=[(i * 256, (i + 1) * 256, otap(h, obase, t0, 256)) for i, h in enumerate(hs)])
                            if sink:
                                u["sink"] = hs
                            units.append(u)
                if l == 0:
                    for ha, hb in ((0, 2), (1, 3), (4, 6), (5, 7)):
                        hf = ha % 2
                        tiles = []
                        for kt_i in range(2):
                            k0 = t0 + kt_i * 128
                            tiles.append(dict(kt=128,
                                              s=[(i * 256, (i + 1) * 256, qap(4 + h // 2, hf, k0, 128), qap(h // 2, hf, t0, 256))
                                                 for i, h in enumerate((ha, hb))],
                                              pv=[(i * 256, (i + 1) * 256, 0, 128, VA[:, s_ * 2 + kt_i, h, :])
                                                  for i, h in enumerate((ha, hb))]))
                        units.append(dict(N=512, tiles=tiles,
                                          outs=[(i * 256, (i + 1) * 256, otap(h, 0, t0, 256)) for i, h in enumerate((ha, hb))]))
                    gqa_units(2, qpos_b, kpos_b, 8, 4, True)
                else:
                    gqa_units(4, qpos_c, kpos_c, 0, 0, False)
        else:
            if l == 0:
                for h, qc in [(2 * c + i, qc) for c in range(4) for qc in range(2) for i in range(2)]:
                    hf = h % 2
                    bslot = BIAS[:, h % 4, :]
                    if True:
                        q0 = qc * 512
                        qa_ = qap(h // 2, hf, q0, 512)
                        tiles = []
                        for t4 in range(4):
                            tiles.append(dict(kt=128, s=[(0, 512, KCT[hf * 64:(hf + 1) * 64, h // 2, t4 * 128:(t4 + 1) * 128], qa_)],
                                              pv=[(0, 512, 0, 128, VC[:, t4, h, :])]))
                        for mp in range(8):
                            zs = nbr_exp_segments(qc, mp)
                            if zs is None:
                                continue
                            e0 = 8 * qc - 2 * mp + 10
                            zv = [z for z in zs if z[2] != 3]
                            c_lo, c_hi = min(z[0] for z in zv), max(z[1] for z in zv)
                            tiles.append(dict(kt=128,
                                              s=[(c_lo, c_hi, qap(4 + h // 2, hf, mp * 128, 128),
                                                  qap(h // 2, hf, q0 + c_lo, c_hi - c_lo))],
                                              mul=(bslot[:, e0 * 64 + c_lo:e0 * 64 + c_hi], "dve"), cr=(c_lo, c_hi),
                                              segs=[z for z in zs if c_lo <= z[0] < c_hi],
                                              pv=[(c_lo, c_hi, 0, 128, VA[:, mp, h, :])]))
                        u = dict(N=512, tiles=tiles, outs=[(0, 512, otap(h, 0, q0, 512))])
                        if qc == 1 and h % 2 == 0 and h < 6:
                            u["prep"] = (lambda h=h: (bias_prep(h + 2), bias_prep(h + 3)))
                        units.append(u)
                for qb, kv in [(qb, kv) for qb in range(8) for kv in range(2)]:
                    hs = [kv * 4 + i for i in range(4)]
                    if True:
                        q0 = qb * 128
                        tiles = []

                        def sl_for(kap):
                            return [(i * 128, (i + 1) * 128, kap, qap(qpos_b(h)[0], qpos_b(h)[1], q0, 128))
                                    for i, h in enumerate(hs)]
                        for t4 in range(4):
                            tiles.append(dict(kt=128, s=sl_for(KCT[kv * 64:(kv + 1) * 64, 4, t4 * 128:(t4 + 1) * 128]),
                                              pv=[(0, 512, 0, 128, VC[:, t4, 8 + kv, :])]))
                        for kb in (qb - 1, qb, qb + 1):
                            if kb < 0 or kb > 7:
                                continue
                            tl = dict(kt=128, s=sl_for(qap(12, kv, kb * 128, 128)),
                                      pv=[(0, 512, 0, 128, VA[:, kb, 8 + kv, :])])
                            if kb == qb - 1:
                                tl["mul"] = (WM[:, 0, :], "dve")
                            elif kb == qb + 1:
                                tl["mul"] = (WM[:, 1, :], "dve")
                            tiles.append(tl)
                        units.append(dict(N=512, tiles=tiles, sink=hs,
                                          outs=[(i * 128, (i + 1) * 128, otap(h, 4, q0, 128)) for i, h in enumerate(hs)]))
            else:
                for h, qc in [(blk + i + 4 * j, qc) for blk in (0, 8) for i in range(4) for qc in range(2) for j in range(2)]:
                    kv = h // 4
                    qch, qhf = qpos_c(h)
                    kch, khf = kpos_c(kv)
                    if True:
                        q0 = qc * 512
                        qa_ = qap(qch, qhf, q0, 512)
                        tiles = []
                        for t4 in range(4):
                            tiles.append(dict(kt=128, s=[(0, 512, KCT[khf * 64:(khf + 1) * 64, kv // 2, t4 * 128:(t4 + 1) * 128], qa_)],
                                              pv=[(0, 512, 0, 128, VC[:, t4, kv, :])]))
                        for mp in range(8):
                            tiles.append(dict(kt=128, s=[(0, 512, qap(kch, khf, mp * 128, 128), qa_)],
                                              pv=[(0, 512, 0, 128, VA[:, mp, kv, :])]))
                        units.append(dict(N=512, tiles=tiles, outs=[(0, 512, otap(h, 0, q0, 512))]))
        if g == 0:
            for ui_, u in enumerate(units):
                u["recip"] = "act"
        for u in units:
            if "sink" in u:
                hs = u["sink"]
                n = u["N"] // len(hs)
                if n == 128:
                    u["extra"] = [(0, u["N"], erow_b,
                                   esinkA[0:1, hs[0]:hs[0] + len(hs), :].rearrange("p a b -> p (a b)"))]
                else:
                    ex = []
                    for i, h in enumerate(hs):
                        for c in range(n // 128):
                            ex.append((i * n + c * 128, i * n + (c + 1) * 128, erow_b, esinkA[0:1, h, :]))
                    u["extra"] = ex
        def between(ui):
            if l == 0 and g == 0:
                modulation_step(1)
        dl_, dr_ = QKT[:, nch - 1, 0:128], QKT[:, 0, 0:512]
        if g == 0 and l == 0:
            attend(units, between, sbanks=(3, 4, 5), lag=2)
        elif g == 0:
            attend(units, between, sbanks=(2, 3, 4, 5), lag=3, obanks=(0, 1, 6))
        else:
            if l == 0:
                attend(units, between, sbanks=(3, 4, 5, 6), lag=2, obanks=(0, 1, 2), paired=True)
            else:
                attend(units, between, sbanks=(2, 3, 4, 5, 6), lag=4)
        if l == 0 and g == 0:
            while modulation_step(1):
                pass
            finalize_mod(1)

        if stop_after == ("attn", l, g):
            return False

        for qd in range(4):
            wv = wload(wsrc(wo, qd * 256, (qd + 1) * 256), 8, 256)
            linear_fm(wv, 2, OT, 8, lambda j, tt_i, p, qd=qd: resid_update(l, 2, g)(qd * 2 + j, tt_i, p))
        stats_flush()
        return True


    def run():
        for g in range(2):
            load_x(g)
            if g == 0:
                prologue_mod0()
                dump("MOD", MOD[:].rearrange("p l w k r -> p (l w k r)"), (128, 192))
            if stop_after == ("load", g):
                return
            for l in range(2):
                adaln(l, 0, g)
                if stop_after == ("norm1", l, g):
                    return
                if not attention(l, g):
                    return
                if stop_after == ("mix", l, g):
                    return
                adaln(l, 1, g)
                ffn(l, g)
                if stop_after == ("ffn", l, g):
                    return
            final_norm(g)

    run()
    dump("xT", xT[:].rearrange("p a b -> p (a b)"), (128, 8 * TOK))
    dump("HO", HO[:].rearrange("p a b -> p (a b)"), (128, 8 * TOK))
    dump("SCR", SCR[:], (128, 24576))
    T.emit()
    es.close()
    return nc, T


def make_in_maps(inp):
    f = lambda a: np.ascontiguousarray(np.asarray(a, dtype=np.float32))
    consts = host_consts()
    wie_ = f(inp["w_in_even"][0])
    cols = list(range(0, 512)) + list(range(512, 1024))
    for c in range(4):
        cols += list(range(1536 + c * 64, 1536 + (c + 1) * 64)) + list(range(1536 + (4 + c) * 64, 1536 + (5 + c) * 64))
    cols += list(range(2048, 2176))
    wfe_ = f(wie_[:, cols])
    wio_ = f(inp["w_in_odd"][0])
    cols_o = []
    for c in range(8):
        h0 = (c // 4) * 8 + c % 4
        cols_o += list(range(h0 * 64, (h0 + 1) * 64)) + list(range((h0 + 4) * 64, (h0 + 5) * 64))
    cols_o += list(range(1024, 1280))
    wfo_ = f(wio_[:, cols_o])
    ir, ic = rpb_gather_index()
    rpb = np.asarray(inp["rpb_a"], np.float32)[0]
    rpbt_ = f(rpb[:, ir, ic].reshape(8, 128, 1408))
    shared = dict(
        wmod=f(inp["w_mod"]), wfe=wfe_, wie=wie_, woe=f(inp["w_out_even"][0]),
        wfo=wfo_, wio=wio_, woo=f(inp["w_out_odd"][0]),
        wgu=f(inp["w_gate_up"]), wdn=f(inp["w_down"]), rpbt=rpbt_,
        bmod=f(np.asarray(inp["b_mod"]).reshape(96, 128)),
        sink=f(np.asarray(inp["sink_b"]).reshape(1, 8)),
        kng=f(np.asarray(inp["k_norm_c"]).reshape(1, 64)),
        **consts,
    )
    ng = np.asarray(inp["norm_gain"], np.float32).reshape(32, 128)
    fg = np.asarray(inp["final_gain"], np.float32).reshape(8, 128)
    cctx = np.asarray(inp["c_ctx"], np.float32).reshape(8, 128)
    qn = np.asarray(inp["q_norm_c"], np.float32).reshape(64)
    kn = np.asarray(inp["k_norm_c"], np.float32).reshape(64)
    maps = []
    xp = np.asarray(inp["x_prompt"], np.float32)
    xs = np.asarray(inp["x_sample"], np.float32)
    for i in range(NCORES):
        cb_ = np.asarray(inp["c"], np.float32)[i].reshape(8, 128)
        vecs = np.concatenate([ng, fg, cctx, cb_, np.concatenate([qn, qn])[None], np.concatenate([kn, kn])[None]], 0)
        m = dict(shared)
        m.update(
            xg0=f(xp[4 * i:4 * i + 4].reshape(TOK, D)), xg1=f(xs[i]),
            cak=f(np.asarray(inp["cache_a_k"])[i, 0].reshape(512, 512)),
            cav=f(np.asarray(inp["cache_a_v"])[i, 0].reshape(512, 512)),
            cbk=f(np.asarray(inp["cache_b_k"])[i, 0].reshape(512, 128)),
            cbv=f(np.asarray(inp["cache_b_v"])[i, 0].reshape(512, 128)),
            cck=f(np.asarray(inp["cache_c_k"])[i, 0].reshape(512, 256)),
            ccv=f(np.asarray(inp["cache_c_v"])[i, 0].reshape(512, 256)),
            vecs=f(vecs),
        )
        maps.append(m)
    return maps


_PROG = {}


def kernel(**inputs):
    if "nc" not in _PROG:
        _PROG["nc"] = build_program()[0]
    nc = _PROG["nc"]
    maps = make_in_maps(inputs)
    res = run_bass_kernel_spmd(nc, maps, core_ids=list(range(NCORES)))
    r = res.results
    y_prompt = np.concatenate([r[i]["yg0"].reshape(4, 256, D) for i in range(NCORES)], 0)
    y_sample = np.stack([r[i]["yg1"] for i in range(NCORES)], 0)

    def st(name, h):
        return np.concatenate([r[i][name].reshape(4, 1, 256, h, 64) for i in range(NCORES)], 0)
    return (y_prompt.astype(np.float32), y_sample.astype(np.float32),
            st("sak", 8), st("sav", 8), st("sbk", 2), st("sbv", 2), st("sck", 4), st("scv", 4))
```

```python
import contextlib
import numpy as np
import concourse.bass as bass
import concourse.mybir as mybir
from concourse.bass_utils import run_bass_kernel_spmd

F32 = mybir.dt.float32
BF16 = mybir.dt.bfloat16
AF = mybir.ActivationFunctionType
ALU = mybir.AluOpType
AX = mybir.AxisListType

ENGS = ("pe", "act", "dve", "pool", "sp")
DMA_RING = 8
NWR = 6
NDUM_CTX = 2
NDUM_LAT = 0
STRICT_SAME_ENGINE = True
NCORES = 8
D = 1024
TOK = 1024
DFF = 2816
FC = 22
EPS = 1e-6
NEG = -30000.0


class Op:
    __slots__ = ("eng", "fn", "deps", "dma", "sig", "sem", "val", "idx", "waits", "clock")

    def __init__(self, eng, fn, dma):
        self.eng = eng
        self.fn = fn
        self.dma = dma
        self.deps = set()
        self.sig = False
        self.sem = None
        self.val = 0
        self.waits = []
        self.clock = None


class Tracker:
    def __init__(self, nc):
        self.nc = nc
        self.ops = []
        self.res = {}
        self.gran = {}
        self.psum = set()
        self.dma_ops = {"sp": [], "pool": [], "act": []}

    def _slots(self, ap):
        if ap is None or not hasattr(ap, "tensor") or not hasattr(ap, "ap"):
            return ()
        sp = str(ap.space).upper()
        if "DRAM" in sp or "HBM" in sp:
            return ()
        t = ap.tensor
        pstep = 1
        for s in list(t.shape)[1:]:
            pstep *= int(s)
        dims = ap.ap
        col0 = int(ap.offset) % pstep
        ext = 1
        for (st, cnt) in dims[1:]:
            ext += (int(cnt) - 1) * abs(int(st))
        g = self.gran.get(t.name, 256)
        return [(t.name, s) for s in range(col0 // g, (col0 + ext - 1) // g + 1)]

    def add(self, eng, fn, reads=(), writes=(), dma=False):
        op = Op(eng, fn, dma)
        idx = len(self.ops)
        op.idx = idx
        deps = set()
        rs = []
        for a in reads:
            rs.extend(self._slots(a))
        ws = []
        for a in writes:
            ws.extend(self._slots(a))
        prs = [k for k in rs if k[0] in self.psum]
        if prs:
            ws = ws + [k for k in prs if k not in ws]
        for k in rs:
            st = self.res.get(k)
            if st is not None and st[0] is not None:
                deps.add(st[0])
        for k in ws:
            st = self.res.get(k)
            if st is not None:
                if st[0] is not None:
                    deps.add(st[0])
                deps.update(st[1].values())
                deps.update(st[2])
        keep = set()
        rset = set(rs)
        for d in deps:
            p = self.ops[d]
            if p.dma or dma:
                keep.add(d)
                continue
            if p.eng == eng:
                if eng == "pe":
                    continue
                if STRICT_SAME_ENGINE:
                    keep.add(d)
                    continue
                raw = False
                for k in rset:
                    st = self.res.get(k)
                    if st is not None and st[0] == d:
                        raw = True
                        break
                if raw:
                    keep.add(d)
                continue
            keep.add(d)
        if dma:
            q = self.dma_ops[eng]
            if len(q) >= DMA_RING:
                keep.add(q[len(q) - DMA_RING])
            q.append(idx)
        op.deps = keep
        for d in keep:
            self.ops[d].sig = True
        for k in ws:
            self.res[k] = [idx, {}, []]
        for k in rs:
            st = self.res.get(k)
            if st is None:
                st = [None, {}, []]
                self.res[k] = st
            if st[0] == idx:
                continue
            if dma:
                st[2].append(idx)
            else:
                st[1][eng] = idx
        self.ops.append(op)
        return idx

    def emit(self):
        nc = self.nc
        es = contextlib.ExitStack()
        sems = {}
        for e in ("pe", "act", "dve", "pool"):
            sems[e] = es.enter_context(nc.semaphore("s_" + e))
        for q in ("sp", "pool", "act"):
            if self.dma_ops[q]:
                for i in range(DMA_RING):
                    sems[(q, i)] = es.enter_context(nc.semaphore("d_%s%d" % (q, i)))
        cnt = {e: 0 for e in ("pe", "act", "dve", "pool")}
        dcnt = {"sp": 0, "pool": 0, "act": 0}
        for op in self.ops:
            if op.dma:
                i = dcnt[op.eng]
                dcnt[op.eng] += 1
                op.sem = (op.eng, i % DMA_RING)
                op.val = 16 * (i // DMA_RING + 1)
                op.sig = True
            elif op.sig:
                cnt[op.eng] += 1
                op.sem = op.eng
                op.val = cnt[op.eng]
        eclock = {e: {} for e in ENGS}
        for op in self.ops:
            ec = eclock[op.eng]
            sel = []
            tmp = dict(ec)
            for d in sorted(op.deps, reverse=True):
                p = self.ops[d]
                if tmp.get(p.sem, 0) >= p.val:
                    continue
                sel.append(p)
                for k, v in p.clock.items():
                    if tmp.get(k, 0) < v:
                        tmp[k] = v
            for q in list(sel):
                oth = dict(ec)
                for p in sel:
                    if p is q:
                        continue
                    for k, v in p.clock.items():
                        if oth.get(k, 0) < v:
                            oth[k] = v
                if oth.get(q.sem, 0) >= q.val:
                    sel.remove(q)
            waits = {}
            for p in sel:
                if waits.get(p.sem, 0) < p.val:
                    waits[p.sem] = p.val
                for k, v in p.clock.items():
                    if ec.get(k, 0) < v:
                        ec[k] = v
            op.waits = list(waits.items())
            c = dict(ec)
            if op.sig:
                c[op.sem] = max(c.get(op.sem, 0), op.val)
            op.clock = c
        streams = {e: [o for o in self.ops if o.eng == e] for e in ENGS}
        final = []
        for q in ("sp", "pool", "act"):
            n = dcnt[q]
            for r in range(DMA_RING):
                m = (n - r + DMA_RING - 1) // DMA_RING if n > r else 0
                if m > 0:
                    final.append(((q, r), 16 * m))
        self.nwaits = sum(len(o.waits) for o in self.ops)
        with es:
            with nc.Block() as block:
                def mk(ename):
                    def body(eng):
                        for o in streams[ename]:
                            for (s, v) in o.waits:
                                eng.wait_ge(sems[s], v)
                            ins = o.fn(eng)
                            if o.sig:
                                ins.then_inc(sems[o.sem], 16 if o.dma else 1)
                        if ename == "sp":
                            for (s, v) in final:
                                eng.wait_ge(sems[s], v)
                    return body
                block.tensor(mk("pe"))
                block.scalar(mk("act"))
                block.vector(mk("dve"))
                block.gpsimd(mk("pool"))
                block.sync(mk("sp"))


def _rs(r):
    return min(max(r - 4, 0), 8)


def _cs(c):
    return min(max(c - 8, 0), 48)


def nbr_exp_segments(qc, mp):
    r0 = 8 * qc
    flags = [tuple(1 if _rs(r0 + i) <= 2 * mp + j <= _rs(r0 + i) + 7 else 0 for j in range(2)) for i in range(8)]
    if not any(f != (0, 0) for f in flags):
        return None
    kind = {(1, 1): 0, (1, 0): 1, (0, 1): 2, (0, 0): 3}
    segs = []
    i = 0
    while i < 8:
        j = i
        while j + 1 < 8 and flags[j + 1] == flags[i]:
            j += 1
        segs.append((i * 64, (j + 1) * 64, kind[flags[i]]))
        i = j + 1
    return segs


def host_consts():
    c = {}
    c["idn"] = np.eye(128, dtype=np.float32)
    cb = np.zeros((128, 640), np.float32)
    cb[:, 0:128] = np.eye(128)
    cb[:, 128:256] = 1.0
    for p in range(128):
        cb[p, 256 + (p // 64) * 64: 256 + (p // 64) * 64 + 64] = 1.0
    for m in range(128):
        d = m % 64
        partner = d + 16 if (d % 32) < 16 else d - 16
        cb[(m // 64) * 64 + partner, 384 + m] = 1.0
    cb[0, 512 + 64:512 + 128] = 1.0
    c["cbh"] = cb
    t = np.arange(1024)
    row = (t // 64).astype(np.float32)
    col = (t % 64).astype(np.float32)
    inv = (np.float32(10000.0) ** (-(np.arange(0, 32, 2, dtype=np.float32)) / np.float32(32))).astype(np.float32)
    C = np.zeros((128, 1024), np.float32)
    S = np.zeros((128, 1024), np.float32)
    for p in range(128):
        d = p % 64
        pos = row if d < 32 else col
        ang = (pos * inv[d % 16]).astype(np.float32)
        C[p] = np.cos(ang)
        S[p] = -np.sin(ang) if (d % 32) < 16 else np.sin(ang)
    c["ropeC"] = C
    c["ropeS"] = S
    wm = np.zeros((128, 2, 4, 128), np.float32)
    b = np.arange(128)[:, None]
    a = np.arange(128)[None, :]
    wm[:, 0] = np.where(b >= a, 1.0, 0.0)[:, None, :]
    wm[:, 1] = np.where(b <= a, 1.0, 0.0)[:, None, :]
    c["wmask"] = wm.reshape(128, 1024)
    nm = np.zeros((128, 22, 64), np.float32)
    for p in range(128):
        cp = p % 64
        for cc in range(64):
            if not (_cs(cc) <= cp <= _cs(cc) + 15):
                nm[p, :, cc] = NEG
    c["nbmask"] = nm.reshape(128, 1408)
    return c


def rpb_gather_index():
    p = np.arange(128)
    jp = (p // 64)[:, None, None]
    cp = (p % 64)[:, None, None]
    e = np.arange(22)[None, :, None]
    cc = np.arange(64)[None, None, :]
    dr = jp - (e - 10) + 0 * cc
    dc = cp - cc + 0 * e
    ir = np.clip(dr + 7, 0, 14)
    ic = np.clip(dc + 15, 0, 30)
    return ir, ic


def build_program(stop_after=None, dbg=()):
    nc = bass.Bass("TRN2", target_bir_lowering=False)
    T = Tracker(nc)
    es = contextlib.ExitStack()

    def din(name, shape):
        return nc.dram_tensor(name, list(shape), F32, kind="ExternalInput").ap()

    def dout(name, shape):
        return nc.dram_tensor(name, list(shape), F32, kind="ExternalOutput").ap()

    xg = [din("xg0", (TOK, D)), din("xg1", (TOK, D))]
    cak = din("cak", (512, 512)); cav = din("cav", (512, 512))
    cbk = din("cbk", (512, 128)); cbv = din("cbv", (512, 128))
    cck = din("cck", (512, 256)); ccv = din("ccv", (512, 256))
    vecs = din("vecs", (58, 128)); bmod = din("bmod", (96, 128))
    sink = din("sink", (1, 8)); kng = din("kng", (1, 64))
    wmod = din("wmod", (2, D, 6 * D))
    wfe = din("wfe", (D, 1664)); wie = din("wie", (D, 2304)); woe = din("woe", (D, D))
    wfo = din("wfo", (D, 1280)); wio = din("wio", (D, 1536)); woo = din("woo", (D, D))
    wgu = din("wgu", (2, D, 2 * DFF)); wdn = din("wdn", (2, DFF, D))
    rpbt = din("rpbt", (8, 128, 1408))
    idn = din("idn", (128, 128)); cbh = din("cbh", (128, 640))
    ropeC_h = din("ropeC", (128, 1024)); ropeS_h = din("ropeS", (128, 1024))
    wmask_h = din("wmask", (128, 1024)); nbmask_h = din("nbmask", (128, 1408))

    yg = [dout("yg0", (TOK, D)), dout("yg1", (TOK, D))]
    sak = dout("sak", (TOK, 512)); sav = dout("sav", (TOK, 512))
    sbk = dout("sbk", (TOK, 128)); sbv = dout("sbv", (TOK, 128))
    sck = dout("sck", (TOK, 256)); scv = dout("scv", (TOK, 256))

    def sb(name, shape, dt, gran=None):
        t = es.enter_context(nc.sbuf_tensor(name, list(shape), dt))
        if gran:
            T.gran[name] = gran
        return t

    def ps(name, shape, dt):
        t = es.enter_context(nc.psum_tensor(name, list(shape), dt))
        T.gran[name] = 1 << 20
        T.psum.add(name)
        return t

    xT = sb("xT", [128, 8, TOK], F32)
    HO = sb("HO", [128, 8, TOK], BF16)
    SCR = sb("SCR", [128, 24576], BF16)
    VC = sb("VC", [128, 4, 10, 128], BF16)
    KCT = sb("KCT", [128, 5, 512], BF16)
    CST = sb("CST", [128, 4, 256], BF16)
    PT = sb("PT", [128, 6, 512], BF16)
    BIAS = sb("BIAS", [128, 4, 1408], BF16, gran=1408)
    NBM = sb("NBM", [128, 1408], BF16)
    WR = [sb("WR%d" % i, [128, 2048], BF16, gran=1 << 20) for i in range(NWR)]
    STG = sb("STG", [128, 4, 1024], F32)
    idf = sb("idf", [128, 128], F32)
    CB = sb("CB", [128, 640], BF16)
    ROC = sb("ROC", [128, 1024], F32)
    ROS = sb("ROS", [128, 1024], F32)
    WM = sb("WM", [128, 2, 512], BF16)
    TMPF = sb("TMPF", [128, 4, 512], F32, gran=512)
    TMPB = sb("TMPB", [128, 4, 512], BF16, gran=512)
    RCP = sb("RCP", [128, 1, 512], F32, gran=512)
    stA = sb("stA", [96, 128], F32)
    stB = sb("stB", [58, 128], F32)
    bmodT = sb("bmodT", [128, 96], F32)
    vecT = sb("vecT", [128, 58], F32)
    sc2b = sb("sc2b", [128, 8, 2], BF16)
    MROW = sb("MROW", [2, 2, 256], F32, gran=256)
    MOD = sb("MOD", [128, 2, 6, 8, 2], F32)
    modr = sb("modr", [128, 2, 48, 2], F32)
    epsT = sb("epsT", [128, 1], F32)
    sinkT = sb("sinkT", [1, 8], F32)
    esinkA = sb("esinkA", [1, 8, 128], BF16)
    kgB = sb("kgB", [128, 64], F32)
    SM = sb("SM", [128, 64], F32)
    BCOL = sb("BCOL", [128, 3], F32)

    PS = [ps("ps%d" % i, [128, 512], F32) for i in range(7)]
    PSB = ps("psb", [128, 1024], BF16)

    ident_b = CB[:, 0:128]
    ones_b = CB[:, 128:256]
    blk_b = CB[:, 256:384]
    perm_b = CB[:, 384:512]
    erow_b = CB[0:1, 512:640]

    def aps(*xs):
        return [x for x in xs if hasattr(x, "tensor")]

    def dma(q, out, in_):
        T.add(q, lambda e: e.dma_start(out=out, in_=in_), reads=aps(in_), writes=aps(out), dma=True)

    def mm(out, lhsT, rhs, start, stop):
        T.add("pe", lambda e: e.matmul(out, lhsT=lhsT, rhs=rhs, start=start, stop=stop,
                                       skip_group_check=True),
              reads=[lhsT, rhs], writes=[out])

    def tr(out, in_, ident):
        T.add("pe", lambda e: e.transpose(out=out, in_=in_, identity=ident), reads=[in_, ident], writes=[out])

    def act(out, in_, func, bias=None, scale=None, eng="act"):
        kw = {}
        if bias is not None:
            kw["bias"] = bias
        if scale is not None:
            kw["scale"] = scale
        T.add("act", lambda e: e.activation(out=out, in_=in_, func=func, **kw),
              reads=aps(in_, bias, scale), writes=[out])

    def tt(out, in0, in1, op, eng="dve"):
        T.add(eng, lambda e: e.tensor_tensor(out=out, in0=in0, in1=in1, op=op), reads=[in0, in1], writes=[out])

    def ts(out, in0, s1, op0, s2=None, op1=None, eng="dve"):
        if op1 is None:
            T.add(eng, lambda e: e.tensor_scalar(out=out, in0=in0, scalar1=s1, scalar2=None, op0=op0),
                  reads=aps(in0, s1), writes=[out])
        else:
            T.add(eng, lambda e: e.tensor_scalar(out=out, in0=in0, scalar1=s1, scalar2=s2, op0=op0, op1=op1),
                  reads=aps(in0, s1, s2), writes=[out])

    def stt(out, in0, scalar, in1, op0, op1, eng="dve"):
        T.add(eng, lambda e: e.scalar_tensor_tensor(out=out, in0=in0, scalar=scalar, in1=in1, op0=op0, op1=op1),
              reads=aps(in0, scalar, in1), writes=[out])

    def cp(out, in_, eng="dve"):
        T.add(eng, lambda e: e.tensor_copy(out=out, in_=in_), reads=[in_], writes=[out])

    def recip(out, in_):
        T.add("dve", lambda e: e.reciprocal(out=out, in_=in_), reads=[in_], writes=[out])

    def recip_act(out, in_):
        act(out, in_, AF.Ln)
        act(out, out, AF.Exp, scale=-1.0)

    def memset(ap, v, eng="dve"):
        T.add(eng, lambda e: e.memset(ap, v), writes=[ap])

    rot = {}

    def nxt(key, n):
        v = rot.get(key, 0)
        rot[key] = v + 1
        return v % n

    def psum_lin():
        return PS[nxt("lin", 2)]

    def psum_aux():
        return PS[2 + nxt("aux", 2)]

    def psum_proj():
        return PS[(0, 1, 4, 5)[nxt("proj", 4)]]

    def psum_s():
        return PS[3 + nxt("s", 3)]

    def tmpf():
        return TMPF[:, nxt("tmpf", 4), :]

    def tmpb():
        return TMPB[:, nxt("tmpb", 4), :]

    wr_state = {"i": 0}

    def wload(src_ap, kc, ncol):
        slot = WR[wr_state["i"] % NWR]
        wr_state["i"] += 1
        view = slot[:, 0:kc * ncol].rearrange("p (k n) -> p k n", k=kc)
        dma("pool", view, src_ap)
        return view

    def wsrc(w2d, c0, c1):
        return w2d.rearrange("(k p) n -> p k n", p=128)[:, :, c0:c1]

    dma("sp", idf[:], idn)
    dma("pool", CB[:], cbh)
    dma("sp", ROC[:], ropeC_h)
    dma("sp", ROS[:], ropeS_h)
    dma("pool", WM[:].rearrange("p a b -> p (a b)"), wmask_h)
    dma("pool", NBM[:], nbmask_h)
    dma("sp", stA[:], bmod)
    dma("sp", stB[:], vecs)
    dma("sp", sinkT[:], sink)
    dma("sp", kgB[:], bass.AP(tensor=kng.tensor, offset=0, ap=[[0, 128], [1, 64]]))
    memset(epsT[:], EPS)
    memset(BCOL[:, 0:2], 0.0)
    memset(BCOL[64:128, 0:1], NEG)
    memset(BCOL[0:64, 1:2], NEG)
    memset(BCOL[:, 2:3], NEG)
    for t4 in range(4):
        memset(VC[:, t4, :, 64:128], 1.0)

    pA = psum_aux()
    tr(pA[:, 0:96], stA[:, :], idf[0:96, 0:96])
    cp(bmodT[:], pA[:, 0:96])
    pB = psum_aux()
    tr(pB[:, 0:58], stB[:, :], idf[0:58, 0:58])
    cp(vecT[:], pB[:, 0:58])
    act(sc2b[:, :, 0], vecT[:, 40:48], AF.Silu)
    act(sc2b[:, :, 1], vecT[:, 48:56], AF.Silu)
    act(sinkT[:], sinkT[:], AF.Exp)
    cp(esinkA[:], sinkT[:].unsqueeze(2).broadcast_to([1, 8, 128]))

    pM = PS[6]

    def mod_part1(l, nt):
        wv = wload(wsrc(wmod[l], nt * 256, (nt + 1) * 256), 8, 256)
        pr = PS[2]
        for kc in range(8):
            mm(pr[0:2, 0:256], sc2b[:, kc, :], wv[:, kc, :], start=(kc == 0), stop=(kc == 7))
        mrow = MROW[0:2, nt % 2, :]
        cp(mrow, pr[0:2, 0:256])

    def mod_part2(l, nt):
        mrow = MROW[0:2, nt % 2, :]
        for j in range(2):
            jj = l * 48 + nt * 2 + j
            tr(pM[:, jj * 2:jj * 2 + 2], mrow[:, j * 128:(j + 1) * 128], idf[0:2, 0:2])

    mod_state = {0: 0, 1: 0}

    def modulation_step(l):
        i = mod_state[l]
        if i > 24:
            return False
        if i < 24:
            mod_part1(l, i)
        if i >= 1:
            mod_part2(l, i - 1)
        mod_state[l] = i + 1
        return True

    def finalize_mod(l):
        tt(modr[:, l], pM[:, l * 96:(l + 1) * 96].rearrange("p (j r) -> p j r", r=2),
           bmodT[:, l * 48:(l + 1) * 48].unsqueeze(2).broadcast_to([128, 48, 2]), ALU.add)
        mr = modr[:, l].rearrange("p (s k) r -> p s k r", s=6)
        for half in range(2):
            g = vecT[:, (l * 2 + half) * 8:(l * 2 + half) * 8 + 8]
            sh, scl, gt = mr[:, 3 * half + 0], mr[:, 3 * half + 1], mr[:, 3 * half + 2]
            ts(MOD[:, l, 3 * half + 0], scl, 1.0, ALU.add)
            tt(MOD[:, l, 3 * half + 0], MOD[:, l, 3 * half + 0], g.unsqueeze(2).broadcast_to([128, 8, 2]), ALU.mult)
            cp(MOD[:, l, 3 * half + 1], sh)
            cp(MOD[:, l, 3 * half + 2], gt)

    def prologue_mod0():
        while modulation_step(0):
            pass
        finalize_mod(0)

    dbg_out = []

    def dump(name, ap, shape):
        if name in dbg:
            o = nc.dram_tensor("dbg_" + name, list(shape), ap.dtype, kind="ExternalOutput").ap()
            dma("sp", o, ap)

    def load_x(g):
        for t in range(8):
            st = STG[:, nxt("stg", 4), :]
            dma("sp", st, xg[g][t * 128:(t + 1) * 128, :])
            for half in range(2):
                p = psum_lin()
                for q in range(4):
                    kc = half * 4 + q
                    tr(p[:, q * 128:(q + 1) * 128], st[:, kc * 128:(kc + 1) * 128], idf[:])
                dst = xT[:, half * 4:half * 4 + 4, t * 128:(t + 1) * 128]
                src = p[:, :].rearrange("p (a b) -> p a b", a=4)
                if half == 0:
                    act(dst, src, AF.Identity)
                else:
                    cp(dst, src)

    ST = (PS[2], PS[3])
    stats = {"ready": False, "pend": []}

    def stats_push(m, tt_i):
        sq = tmpb()
        act(sq, xT[:, m, tt_i * 512:(tt_i + 1) * 512], AF.Square)
        stats["pend"].append((m, tt_i, sq))
        if len(stats["pend"]) > 1:
            stats_pop()

    def stats_pop():
        m, tt_i, sq = stats["pend"].pop(0)
        mm(ST[tt_i][:, :], ones_b, sq, start=(m == 0), stop=(m == 7))

    def stats_flush():
        while stats["pend"]:
            stats_pop()
        stats["ready"] = True

    def rstd_banks():
        if not stats["ready"]:
            for m in range(8):
                for tt_i in range(2):
                    stats_push(m, tt_i)
            stats_flush()
        for tt_i in range(2):
            act(ST[tt_i][:, :], ST[tt_i][:, :], AF.Ln, bias=epsT[:, 0:1], scale=1.0 / D)
            act(ST[tt_i][:, :], ST[tt_i][:, :], AF.Exp, scale=-0.5)
        stats["ready"] = False

    def adaln(l, half, r_i):
        rstd_banks()
        for tt_i in range(2):
            for kc in range(8):
                tmp = tmpf()
                tt(tmp, xT[:, kc, tt_i * 512:(tt_i + 1) * 512], ST[tt_i][:, :], ALU.mult)
                act(HO[:, kc, tt_i * 512:(tt_i + 1) * 512], tmp, AF.Identity,
                    scale=MOD[:, l, 3 * half, kc, r_i:r_i + 1], bias=MOD[:, l, 3 * half + 1, kc, r_i:r_i + 1])

    def final_norm(g):
        rstd_banks()
        for tt_i in range(2):
            for kc in range(8):
                sl = xT[:, kc, tt_i * 512:(tt_i + 1) * 512]
                stt(sl, sl, vecT[:, 32 + kc:33 + kc], ST[tt_i][:, :], ALU.mult, ALU.mult)
        for t in range(8):
            st = STG[:, nxt("stg", 4), :]
            for half in range(2):
                p = psum_lin()
                for q in range(4):
                    kc = half * 4 + q
                    tr(p[:, q * 128:(q + 1) * 128], xT[:, kc, t * 128:(t + 1) * 128], idf[:])
                if half == 0:
                    act(st[:, 0:512], p[:, :], AF.Identity)
                else:
                    cp(st[:, 512:1024], p[:, :])
            dma("sp", yg[g][t * 128:(t + 1) * 128, :], st)

    def linear_fm(wview, ncols_chunks, src, kcn, post, palloc=None):
        for j in range(ncols_chunks):
            for tt_i in range(2):
                p = (palloc or psum_lin)()
                for kc in range(kcn):
                    mm(p[:, :], wview[:, kc, j * 128:(j + 1) * 128], src[:, kc, tt_i * 512:(tt_i + 1) * 512],
                       start=(kc == 0), stop=(kc == kcn - 1))
                post(j, tt_i, p)

    def resid_update(l, w, r_i):
        def post(m, tt_i, p):
            sl = xT[:, m, tt_i * 512:(tt_i + 1) * 512]
            stt(sl, p[:, :], MOD[:, l, w, m, r_i:r_i + 1], sl, ALU.mult, ALU.add)
            stats_push(m, tt_i)
        return post

    def ffn(l, r_i):
        actT = SCR[:, 0:FC * TOK].rearrange("p (k n) -> p k n", k=FC)
        for jj in range(11):
            wg = wload(wsrc(wgu[l], jj * 256, jj * 256 + 256), 8, 256)
            wu = wload(wsrc(wgu[l], DFF + jj * 256, DFF + jj * 256 + 256), 8, 256)
            for j in range(2):
                jg = jj * 2 + j
                for tt_i in range(2):
                    pg = psum_lin()
                    pu = psum_aux()
                    for kc in range(8):
                        mm(pg[:, :], wg[:, kc, j * 128:(j + 1) * 128], HO[:, kc, tt_i * 512:(tt_i + 1) * 512],
                           start=(kc == 0), stop=(kc == 7))
                    for kc in range(8):
                        mm(pu[:, :], wu[:, kc, j * 128:(j + 1) * 128], HO[:, kc, tt_i * 512:(tt_i + 1) * 512],
                           start=(kc == 0), stop=(kc == 7))
                    sg = tmpf()
                    act(sg, pg[:, :], AF.Silu)
                    tt(actT[:, jg, tt_i * 512:(tt_i + 1) * 512], sg, pu[:, :], ALU.mult)
        wdv = wdn[l].rearrange("(k p) n -> p k n", p=128)
        for m in range(8):
            wd0 = wload(wdv[:, 0:11, m * 128:(m + 1) * 128], 11, 128)
            wd1 = wload(wdv[:, 11:22, m * 128:(m + 1) * 128], 11, 128)
            for tt_i in range(2):
                p = psum_lin()
                for k in range(FC):
                    wd = wd0 if k < 11 else wd1
                    mm(p[:, :], wd[:, k % 11, :], actT[:, k, tt_i * 512:(tt_i + 1) * 512], start=(k == 0), stop=(k == FC - 1))
                resid_update(l, 5, r_i)(m, tt_i, p)
        stats_flush()

    qk_pipe = []

    def qk_stage_a(it):
        p = it["p"]
        if it["gain"] is not None:
            it["sq"] = tmpb()
            act(it["sq"], p, AF.Square)
        elif it["rope"]:
            it["cur"] = tmpb()
            act(it["cur"], p, AF.Identity)
        else:
            act(it["dst"], p, AF.Identity)
            it["done"] = True

    def qk_stage_b(it):
        if it.get("done"):
            return
        if it["gain"] is not None:
            pq = psum_aux()
            mm(pq[:, :], blk_b, it["sq"], start=True, stop=True)
            sd = tmpf()
            act(sd, pq[:, :], AF.Ln, bias=epsT[:, 0:1], scale=1.0 / 64)
            act(sd, sd, AF.Exp, scale=-0.5)
            tgt = tmpb() if it["rope"] else it["dst"]
            stt(tgt, it["p"], it["gain"], sd, ALU.mult, ALU.mult)
            it["cur"] = tgt
            if not it["rope"]:
                it["done"] = True

    def qk_stage_c(it):
        if it.get("done"):
            return
        tt_i = it["tt"]
        pr = psum_aux()
        mm(pr[:, :], perm_b, it["cur"], start=True, stop=True)
        t1 = tmpf()
        tt(t1, it["cur"], ROC[:, tt_i * 512:(tt_i + 1) * 512], ALU.mult)
        t2 = tmpf()
        tt(t2, pr[:, :], ROS[:, tt_i * 512:(tt_i + 1) * 512], ALU.mult)
        tt(it["dst"], t1, t2, ALU.add)
        it["done"] = True

    def qk_post(p, dst, tt_i, norm_gain_col, rope):
        it = dict(p=p, dst=dst, tt=tt_i, gain=norm_gain_col, rope=rope)
        qk_pipe.append(it)
        qk_stage_a(it)
        if len(qk_pipe) >= 2:
            qk_stage_b(qk_pipe[-2])
        if len(qk_pipe) >= 3:
            qk_stage_c(qk_pipe[-3])

    def qk_flush():
        n = len(qk_pipe)
        if n >= 1:
            qk_stage_b(qk_pipe[-1])
        if n >= 2:
            qk_stage_c(qk_pipe[-2])
        if n >= 1:
            qk_stage_c(qk_pipe[-1])
        del qk_pipe[:]

    def attend(units, between=None, sbanks=(3, 4, 5), lag=2, obanks=(0, 1), dummy=None, paired=False):
        flat = []
        for ui, u in enumerate(units):
            for ti, tl in enumerate(u["tiles"]):
                flat.append((ui, ti))
        obank = {}
        pend = []

        def do_pv(item):
            ui, ti, ptile = item
            u = units[ui]
            tl = u["tiles"][ti]
            N = u["N"]
            if ti == 0:
                obank[ui] = PS[obanks[nxt("o", len(obanks))]]
            po = obank[ui]
            first = (ti == 0)
            for (c0, c1, pl, ph, vl) in tl["pv"]:
                mm(po[:, c0:c1], vl[pl:ph, :], ptile[pl:ph, c0:c1], start=first, stop=True)
                first = False
            if ti == len(u["tiles"]) - 1:
                for (c0, c1, lh, rh) in u.get("extra", []):
                    mm(po[:, c0:c1], lh, rh, start=False, stop=True)
                rc = RCP[:, 0, :]
                if u.get("recip") == "act":
                    recip_act(rc[0:64, 0:N], po[64:128, 0:N])
                else:
                    recip(rc[0:64, 0:N], po[64:128, 0:N])
                for (c0, c1, dst) in u["outs"]:
                    tt(dst, po[0:64, c0:c1], rc[0:64, c0:c1], ALU.mult)

        groups = []
        if paired:
            for i in range(0, len(units), 2):
                na, nb = len(units[i]["tiles"]), len(units[i + 1]["tiles"])
                assert na == nb
                for ti in range(na):
                    groups.append([(i, ti), (i + 1, ti)])
        else:
            groups = [[it] for it in flat]

        for grp in groups:
            infos = []
            slists = []
            for (ui, ti) in grp:
                u = units[ui]
                if ti == 0 and "prep" in u:
                    u["prep"]()
                tl = u["tiles"][ti]
                pS = PS[sbanks[nxt("s", len(sbanks))]]
                infos.append((ui, ti, pS))
                ops = [(pS, 0, u["N"], lh, rh) for (lh, rh) in tl.get("pre", [])]
                ops += [(pS, c0, c1, lh, rh) for (c0, c1, lh, rh) in tl["s"]]
                ops += [(pS, 0, u["N"], lh, rh) for (lh, rh) in tl.get("post", [])]
                slists.append((tl["kt"], ops))
            started = set()
            for k in range(max(len(o) for (_, o) in slists)):
                for (kt, ops) in slists:
                    if k < len(ops):
                        pS, c0, c1, lh, rh = ops[k]
                        mm(pS[0:kt, c0:c1], lh, rh, start=(pS.name not in started), stop=True)
                        started.add(pS.name)
            for (ui, ti, pS) in infos:
                u = units[ui]
                tl = u["tiles"][ti]
                N = u["N"]
                kt = tl["kt"]
                ptile = PT[:, nxt("pt", 6), :]
                if "segs" in tl:
                    for (c0, c1, kind) in tl["segs"]:
                        if kind == 0:
                            act(ptile[0:kt, c0:c1], pS[0:kt, c0:c1], AF.Exp, scale=0.125)
                        else:
                            act(ptile[0:kt, c0:c1], pS[0:kt, c0:c1], AF.Exp, scale=0.125, bias=BCOL[:, kind - 1:kind])
                else:
                    act(ptile[0:kt, 0:N], pS[0:kt, 0:N], AF.Exp, scale=0.125)
                if "mul" in tl:
                    mul_ap, mul_eng = tl["mul"]
                    m0, m1 = tl.get("cr", (0, N))
                    tt(ptile[0:kt, m0:m1], ptile[0:kt, m0:m1], mul_ap, ALU.mult, eng=mul_eng)
                pend.append((ui, ti, ptile))
            while len(pend) > lag:
                do_pv(pend.pop(0))
            for (ui, ti, pS) in infos:
                if between is not None and ti == len(units[ui]["tiles"]) - 1:
                    between(ui)
        while pend:
            do_pv(pend.pop(0))

    def attention(l, g):
        nch = 13 if l == 0 else 10
        nkv = 10 if l == 0 else 4
        QKT = SCR[:, 0:nch * TOK].rearrange("p (c n) -> p c n", c=nch)
        VA = SCR[:, 14336:14336 + 8 * nkv * 128].rearrange("p (t h d) -> p t h d", t=8, h=nkv)
        for t8 in range(8):
            memset(VA[:, t8, :, 64:128], 1.0)
        wf = wfe if l == 0 else wfo
        wi = wie if l == 0 else wio
        wo = woe if l == 0 else woo
        ncolf = nch * 128

        def bias_prep(h):
            bslot = BIAS[:, h % 4, :]
            dma("pool", bslot, rpbt[h])
            tt(bslot, bslot, NBM[:], ALU.add)
            act(bslot, bslot, AF.Exp)

        def post_fm(c0):
            def post(j, tt_i, p):
                ch = c0 + j
                dst = QKT[:, ch, tt_i * 512:(tt_i + 1) * 512]
                if l == 0:
                    rope = (g == 1 and ch >= 8)
                    qk_post(p[:, :], dst, tt_i, None, rope)
                else:
                    gcol = vecT[:, 56:57] if ch < 8 else vecT[:, 57:58]
                    qk_post(p[:, :], dst, tt_i, gcol, g == 1)
            return post

        c0 = 0
        while c0 < nch:
            n = min(2, nch - c0)
            wv = wload(wsrc(wf, c0 * 128, (c0 + n) * 128), 8, n * 128)
            linear_fm(wv, n, HO, 8, post_fm(c0), palloc=psum_proj)
            c0 += n
        qk_flush()

        if g == 1:
            if l == 0:
                bias_prep(0)
                bias_prep(1)
                for hc in range(2):
                    dma("pool", CST[:], cak.rearrange("(t p) f -> p t f", p=128)[:, :, hc * 256:(hc + 1) * 256])
                    for c2 in range(2):
                        for t4 in range(4):
                            tr(PSB[:, t4 * 128:(t4 + 1) * 128], CST[:, t4, c2 * 128:(c2 + 1) * 128], ident_b)
                        cp(KCT[:, hc * 2 + c2, :], PSB[:, 0:512])
                dma("pool", CST[:, :, 0:128], cbk.rearrange("(t p) f -> p t f", p=128))
                for t4 in range(4):
                    tr(PSB[:, 512 + t4 * 128:512 + (t4 + 1) * 128], CST[:, t4, 0:128], ident_b)
                cp(KCT[:, 4, :], PSB[:, 512:1024])
                for t4 in range(4):
                    dma("pool", VC[:, t4, 0:8, 0:64], cav[t4 * 128:(t4 + 1) * 128, :].rearrange("p (h d) -> p h d", d=64))
                    dma("pool", VC[:, t4, 8:10, 0:64], cbv[t4 * 128:(t4 + 1) * 128, :].rearrange("p (h d) -> p h d", d=64))
            else:
                dma("pool", CST[:, :, 0:256], cck.rearrange("(t p) f -> p t f", p=128))
                for c in range(2):
                    for t4 in range(4):
                        tr(PSB[:, t4 * 128:(t4 + 1) * 128], CST[:, t4, c * 128:(c + 1) * 128], ident_b)
                    cp(KCT[:, c, :], PSB[:, 0:512])
                for t4 in range(4):
                    dma("pool", VC[:, t4, 0:4, 0:64], ccv[t4 * 128:(t4 + 1) * 128, :].rearrange("p (h d) -> p h d", d=64))


        if stop_after == ("fm", l, g):
            return False

        def tokmaj(c0, ncol, post):
            wvs = []
            cc = 0
            while cc < ncol:
                n = min(256, ncol - cc)
                wvs.append((cc, n, wload(wsrc(wi, c0 + cc, c0 + cc + n), 8, n)))
                cc += n
            for t in range(8):
                p = psum_lin()
                for (cc, n, wv) in wvs:
                    for kc in range(8):
                        mm(p[:, cc:cc + n], HO[:, kc, t * 128:(t + 1) * 128], wv[:, kc, 0:n], start=(kc == 0 and cc == 0), stop=(kc == 7))
                post(t, p)

        if l == 0:
            if g == 1:
                tokmaj(1024, 512, lambda t, p: cp(VA[:, t, 0:8, 0:64], p[:, 0:512].rearrange("p (h d) -> p h d", d=64)))
                tokmaj(2176, 128, lambda t, p: cp(VA[:, t, 8:10, 0:64], p[:, 0:128].rearrange("p (h d) -> p h d", d=64)))
            else:
                def post_k(out_d, ncol):
                    def post(t, p):
                        st = STG[:, nxt("stg", 4), :]
                        act(st[:, 0:ncol], p[:, 0:ncol], AF.Identity)
                        dma("sp", out_d[t * 128:(t + 1) * 128, :], st[:, 0:ncol])
                    return post

                def post_v(out_d, ncol, h0, h1):
                    def post(t, p):
                        st = STG[:, nxt("stg", 4), :]
                        act(st[:, 0:ncol], p[:, 0:ncol], AF.Identity)
                        dma("sp", out_d[t * 128:(t + 1) * 128, :], st[:, 0:ncol])
                        cp(VA[:, t, h0:h1, 0:64], p[:, 0:ncol].rearrange("p (h d) -> p h d", d=64))
                    return post
                tokmaj(512, 512, post_k(sak, 512))
                tokmaj(1024, 512, post_v(sav, 512, 0, 8))
                tokmaj(2048, 128, post_k(sbk, 128))
                tokmaj(2176, 128, post_v(sbv, 128, 8, 10))
        else:
            if g == 1:
                tokmaj(1280, 256, lambda t, p: cp(VA[:, t, 0:4, 0:64], p[:, 0:256].rearrange("p (h d) -> p h d", d=64)))
            else:
                def post_kv(t, p):
                    st = STG[:, nxt("stg", 4), :]
                    act(st[:, 256:512], p[:, 256:512], AF.Identity)
                    dma("sp", scv[t * 128:(t + 1) * 128, :], st[:, 256:512])
                    cp(VA[:, t, 0:4, 0:64], p[:, 256:512].rearrange("p (h d) -> p h d", d=64))
                    kf = st[:, 0:256]
                    act(kf, p[:, 0:256], AF.Identity)
                    sq = st[:, 512:768]
                    tt(sq, kf, kf, ALU.mult)
                    ss = SM[:, 0:4]
                    T.add("dve", lambda e: e.tensor_reduce(out=ss, in_=sq.rearrange("p (h d) -> p h d", d=64),
                                                           axis=AX.X, op=ALU.add),
                          reads=[sq], writes=[ss])
                    sd = SM[:, 8:12]
                    act(sd, ss, AF.Sqrt, bias=epsT[:, 0:1], scale=1.0 / 64)
                    recip(sd, sd)
                    kn = st[:, 768:1024]
                    tt(kn.rearrange("p (h d) -> p h d", d=64), kf.rearrange("p (h d) -> p h d", d=64),
                       sd.unsqueeze(2).broadcast_to([128, 4, 64]), ALU.mult)
                    tt(kn.rearrange("p (h d) -> p h d", d=64), kn.rearrange("p (h d) -> p h d", d=64),
                       kgB[:].unsqueeze(1).broadcast_to([128, 4, 64]), ALU.mult)
                    dma("sp", sck[t * 128:(t + 1) * 128, :], kn)
                tokmaj(1024, 512, post_kv)

        if stop_after == ("proj", l, g):
            return False

        OT = HO
        units = []

        def qap(ch, half, t0, n):
            return QKT[half * 64:(half + 1) * 64, ch, t0:t0 + n]

        def otap(h, chunk0, t0, n):
            return OT[(h % 2) * 64:(h % 2 + 1) * 64, chunk0 + h // 2, t0:t0 + n]

        if l == 0:
            qpos_b = lambda h: (8 + h % 4, h // 4)
            kpos_b = lambda j: (12, j)
        else:
            qpos_c = lambda h: ((h // 8) * 4 + h % 4, (h // 4) % 2)
            kpos_c = lambda j: (8 + j // 2, j % 2)

        if g == 0:
            for s_ in range(4):
                t0 = s_ * 256

                def gqa_units(nkvh, qpos, kpos, vbase, obase, sink):
                    for kv in range(nkvh):
                        kch, khf = kpos(kv)
                        for pr in range(2):
                            hs = [kv * 4 + pr * 2, kv * 4 + pr * 2 + 1]
                            tiles = []
                            for kt_i in range(2):
                                k0 = t0 + kt_i * 128
                                sl = []
                                for i, h in enumerate(hs):
                                    qch, qhf = qpos(h)
                                    sl.append((i * 256, (i + 1) * 256, qap(kch, khf, k0, 128), qap(qch, qhf, t0, 256)))
                                tiles.append(dict(kt=128, s=sl, pv=[(0, 512, 0, 128, VA[:, s_ * 2 + kt_i, vbase + kv, :])]))
                            u = dict(N=512, tiles=tiles,
                                     outs=[(i * 256, (i + 1) * 256, otap(h, obase, t0, 256)) for i, h in enumerate(hs)])
                            if sink:
                                u["sink"] = hs
                            units.append(u)
                if l == 0:
                    for ha, hb in ((0, 2), (1, 3), (4, 6), (5, 7)):
                        hf = ha % 2
                        tiles = []
                        for kt_i in range(2):
                            k0 = t0 + kt_i * 128
                            tiles.append(dict(kt=128,
                                              s=[(i * 256, (i + 1) * 256, qap(4 + h // 2, hf, k0, 128), qap(h // 2, hf, t0, 256))
                                                 for i, h in enumerate((ha, hb))],
                                              pv=[(i * 256, (i + 1) * 256, 0, 128, VA[:, s_ * 2 + kt_i, h, :])
                                                  for i, h in enumerate((ha, hb))]))
                        units.append(dict(N=512, tiles=tiles,
                                          outs=[(i * 256, (i + 1) * 256, otap(h, 0, t0, 256)) for i, h in enumerate((ha, hb))]))
                    gqa_units(2, qpos_b, kpos_b, 8, 4, True)
                else:
                    gqa_units(4, qpos_c, kpos_c, 0, 0, False)
        else:
            if l == 0:
                for h, qc in [(2 * c + i, qc) for c in range(4) for qc in range(2) for i in range(2)]:
                    hf = h % 2
                    bslot = BIAS[:, h % 4, :]
                    if True:
                        q0 = qc * 512
                        qa_ = qap(h // 2, hf, q0, 512)
                        tiles = []
                        for t4 in range(4):
                            tiles.append(dict(kt=128, s=[(0, 512, KCT[hf * 64:(hf + 1) * 64, h // 2, t4 * 128:(t4 + 1) * 128], qa_)],
                                              pv=[(0, 512, 0, 128, VC[:, t4, h, :])]))
                        for mp in range(8):
                            zs = nbr_exp_segments(qc, mp)
                            if zs is None:
                                continue
                            e0 = 8 * qc - 2 * mp + 10
                            zv = [z for z in zs if z[2] != 3]
                            c_lo, c_hi = min(z[0] for z in zv), max(z[1] for z in zv)
                            tiles.append(dict(kt=128,
                                              s=[(c_lo, c_hi, qap(4 + h // 2, hf, mp * 128, 128),
                                                  qap(h // 2, hf, q0 + c_lo, c_hi - c_lo))],
                                              mul=(bslot[:, e0 * 64 + c_lo:e0 * 64 + c_hi], "dve"), cr=(c_lo, c_hi),
                                              segs=[z for z in zs if c_lo <= z[0] < c_hi],
                                              pv=[(c_lo, c_hi, 0, 128, VA[:, mp, h, :])]))
                        u = dict(N=512, tiles=tiles, outs=[(0, 512, otap(h, 0, q0, 512))])
                        if qc == 1 and h % 2 == 0 and h < 6:
                            u["prep"] = (lambda h=h: (bias_prep(h + 2), bias_prep(h + 3)))
                        units.append(u)
                for qb, kv in [(qb, kv) for qb in range(8) for kv in range(2)]:
                    hs = [kv * 4 + i for i in range(4)]
                    if True:
                        q0 = qb * 128
                        tiles = []

                        def sl_for(kap):
                            return [(i * 128, (i + 1) * 128, kap, qap(qpos_b(h)[0], qpos_b(h)[1], q0, 128))
                                    for i, h in enumerate(hs)]
                        for t4 in range(4):
                            tiles.append(dict(kt=128, s=sl_for(KCT[kv * 64:(kv + 1) * 64, 4, t4 * 128:(t4 + 1) * 128]),
                                              pv=[(0, 512, 0, 128, VC[:, t4, 8 + kv, :])]))
                        for kb in (qb - 1, qb, qb + 1):
                            if kb < 0 or kb > 7:
                                continue
                            tl = dict(kt=128, s=sl_for(qap(12, kv, kb * 128, 128)),
                                      pv=[(0, 512, 0, 128, VA[:, kb, 8 + kv, :])])
                            if kb == qb - 1:
                                tl["mul"] = (WM[:, 0, :], "dve")
                            elif kb == qb + 1:
                                tl["mul"] = (WM[:, 1, :], "dve")
                            tiles.append(tl)
                        units.append(dict(N=512, tiles=tiles, sink=hs,
                                          outs=[(i * 128, (i + 1) * 128, otap(h, 4, q0, 128)) for i, h in enumerate(hs)]))
            else:
                for h, qc in [(blk + i + 4 * j, qc) for blk in (0, 8) for i in range(4) for qc in range(2) for j in range(2)]:
                    kv = h // 4
                    qch, qhf = qpos_c(h)
                    kch, khf = kpos_c(kv)
                    if True:
                        q0 = qc * 512
                        qa_ = qap(qch, qhf, q0, 512)
                        tiles = []
                        for t4 in range(4):
                            tiles.append(dict(kt=128, s=[(0, 512, KCT[khf * 64:(khf + 1) * 64, kv // 2, t4 * 128:(t4 + 1) * 128], qa_)],
                                              pv=[(0, 512, 0, 128, VC[:, t4, kv, :])]))
                        for mp in range(8):
                            tiles.append(dict(kt=128, s=[(0, 512, qap(kch, khf, mp * 128, 128), qa_)],
                                              pv=[(0, 512, 0, 128, VA[:, mp, kv, :])]))
                        units.append(dict(N=512, tiles=tiles, outs=[(0, 512, otap(h, 0, q0, 512))]))
        if g == 0:
            for ui_, u in enumerate(units):
                u["recip"] = "act"
        for u in units:
            if "sink" in u:
                hs = u["sink"]
                n = u["N"] // len(hs)
                if n == 128:
                    u["extra"] = [(0, u["N"], erow_b,
                                   esinkA[0:1, hs[0]:hs[0] + len(hs), :].rearrange("p a b -> p (a b)"))]
                else:
                    ex = []
                    for i, h in enumerate(hs):
                        for c in range(n // 128):
                            ex.append((i * n + c * 128, i * n + (c + 1) * 128, erow_b, esinkA[0:1, h, :]))
                    u["extra"] = ex
        def between(ui):
            if l == 0 and g == 0:
                modulation_step(1)
        dl_, dr_ = QKT[:, nch - 1, 0:128], QKT[:, 0, 0:512]
        if g == 0 and l == 0:
            attend(units, between, sbanks=(3, 4, 5), lag=2)
        elif g == 0:
            attend(units, between, sbanks=(2, 3, 4, 5), lag=3)
        else:
            if l == 0:
                attend(units, between, sbanks=(3, 4, 5, 6), lag=2, obanks=(0, 1, 2), paired=True)
            else:
                attend(units, between, sbanks=(2, 3, 4, 5, 6), lag=4)
        if l == 0 and g == 0:
            while modulation_step(1):
                pass
            finalize_mod(1)

        if stop_after == ("attn", l, g):
            return False

        for qd in range(4):
            wv = wload(wsrc(wo, qd * 256, (qd + 1) * 256), 8, 256)
            linear_fm(wv, 2, OT, 8, lambda j, tt_i, p, qd=qd: resid_update(l, 2, g)(qd * 2 + j, tt_i, p))
        stats_flush()
        return True


    def run():
        for g in range(2):
            load_x(g)
            if g == 0:
                prologue_mod0()
                dump("MOD", MOD[:].rearrange("p l w k r -> p (l w k r)"), (128, 192))
            if stop_after == ("load", g):
                return
            for l in range(2):
                adaln(l, 0, g)
                if stop_after == ("norm1", l, g):
                    return
                if not attention(l, g):
                    return
                if stop_after == ("mix", l, g):
                    return
                adaln(l, 1, g)
                ffn(l, g)
                if stop_after == ("ffn", l, g):
                    return
            final_norm(g)

    run()
    dump("xT", xT[:].rearrange("p a b -> p (a b)"), (128, 8 * TOK))
    dump("HO", HO[:].rearrange("p a b -> p (a b)"), (128, 8 * TOK))
    dump("SCR", SCR[:], (128, 24576))
    T.emit()
    es.close()
    return nc, T


def make_in_maps(inp):
    f = lambda a: np.ascontiguousarray(np.asarray(a, dtype=np.float32))
    consts = host_consts()
    wie_ = f(inp["w_in_even"][0])
    cols = list(range(0, 512)) + list(range(512, 1024))
    for c in range(4):
        cols += list(range(1536 + c * 64, 1536 + (c + 1) * 64)) + list(range(1536 + (4 + c) * 64, 1536 + (5 + c) * 64))
    cols += list(range(2048, 2176))
    wfe_ = f(wie_[:, cols])
    wio_ = f(inp["w_in_odd"][0])
    cols_o = []
    for c in range(8):
        h0 = (c // 4) * 8 + c % 4
        cols_o += list(range(h0 * 64, (h0 + 1) * 64)) + list(range((h0 + 4) * 64, (h0 + 5) * 64))
    cols_o += list(range(1024, 1280))
    wfo_ = f(wio_[:, cols_o])
    ir, ic = rpb_gather_index()
    rpb = np.asarray(inp["rpb_a"], np.float32)[0]
    rpbt_ = f(rpb[:, ir, ic].reshape(8, 128, 1408))
    shared = dict(
        wmod=f(inp["w_mod"]), wfe=wfe_, wie=wie_, woe=f(inp["w_out_even"][0]),
        wfo=wfo_, wio=wio_, woo=f(inp["w_out_odd"][0]),
        wgu=f(inp["w_gate_up"]), wdn=f(inp["w_down"]), rpbt=rpbt_,
        bmod=f(np.asarray(inp["b_mod"]).reshape(96, 128)),
        sink=f(np.asarray(inp["sink_b"]).reshape(1, 8)),
        kng=f(np.asarray(inp["k_norm_c"]).reshape(1, 64)),
        **consts,
    )
    ng = np.asarray(inp["norm_gain"], np.float32).reshape(32, 128)
    fg = np.asarray(inp["final_gain"], np.float32).reshape(8, 128)
    cctx = np.asarray(inp["c_ctx"], np.float32).reshape(8, 128)
    qn = np.asarray(inp["q_norm_c"], np.float32).reshape(64)
    kn = np.asarray(inp["k_norm_c"], np.float32).reshape(64)
    maps = []
    xp = np.asarray(inp["x_prompt"], np.float32)
    xs = np.asarray(inp["x_sample"], np.float32)
    for i in range(NCORES):
        cb_ = np.asarray(inp["c"], np.float32)[i].reshape(8, 128)
        vecs = np.concatenate([ng, fg, cctx, cb_, np.concatenate([qn, qn])[None], np.concatenate([kn, kn])[None]], 0)
        m = dict(shared)
        m.update(
            xg0=f(xp[4 * i:4 * i + 4].reshape(TOK, D)), xg1=f(xs[i]),
            cak=f(np.asarray(inp["cache_a_k"])[i, 0].reshape(512, 512)),
            cav=f(np.asarray(inp["cache_a_v"])[i, 0].reshape(512, 512)),
            cbk=f(np.asarray(inp["cache_b_k"])[i, 0].reshape(512, 128)),
            cbv=f(np.asarray(inp["cache_b_v"])[i, 0].reshape(512, 128)),
            cck=f(np.asarray(inp["cache_c_k"])[i, 0].reshape(512, 256)),
            ccv=f(np.asarray(inp["cache_c_v"])[i, 0].reshape(512, 256)),
            vecs=f(vecs),
        )
        maps.append(m)
    return maps


_PROG = {}


def kernel(**inputs):
    if "nc" not in _PROG:
        _PROG["nc"] = build_program()[0]
    nc = _PROG["nc"]
    maps = make_in_maps(inputs)
    res = run_bass_kernel_spmd(nc, maps, core_ids=list(range(NCORES)))
    r = res.results
    y_prompt = np.concatenate([r[i]["yg0"].reshape(4, 256, D) for i in range(NCORES)], 0)
    y_sample = np.stack([r[i]["yg1"] for i in range(NCORES)], 0)

    def st(name, h):
        return np.concatenate([r[i][name].reshape(4, 1, 256, h, 64) for i in range(NCORES)], 0)
    return (y_prompt.astype(np.float32), y_sample.astype(np.float32),
            st("sak", 8), st("sav", 8), st("sbk", 2), st("sbv", 2), st("sck", 4), st("scv", 4))
```
